# Optimizing a Trainium2 kernel written in Bass

```python
import math
import jax
import jax.numpy as jnp
from jax import lax
import numpy as np


D_MODEL = 1024
BATCH = 8
SEQ = 8192
DEPTH = 2
DEC_BATCH = 2
DEC_SEQ = 16384
PAST_LEN = 128

ROPE_THETA = 500000.0
NORM_EPS = 1e-6
NEG_INF = -1e30

A_HEADS = D_MODEL // 128
A_HEAD_DIM = 64
A_ROPE_DIM = A_HEAD_DIM // 4
A_PATTERNS = ((128, 1), (512, 4), (2048, 16))
A_BLOCK = 64
A_WIDTH = A_HEADS * A_HEAD_DIM

B_HEADS = D_MODEL // 128
B_NOPE_DIM = 64
B_ROPE_DIM = 32
B_QK_DIM = B_NOPE_DIM + B_ROPE_DIM
B_V_DIM = 64
B_Q_RANK = D_MODEL // 4
B_KV_RANK = D_MODEL // 8
B_QBLOCK = 128
B_WIDTH = B_HEADS * B_V_DIM

C_WIDTH = D_MODEL // 2
C_BLOCKS = 8
C_BLOCK_DIM = C_WIDTH // C_BLOCKS
C_CONV = 4
C_GATE_C = 8.0

D_HEADS = 4
D_HEAD_DIM = 128
D_WIDTH = D_HEADS * D_HEAD_DIM
D_CONV = 4
D_CHUNK = 64

N_BRANCH = 4
MIX_WIDTH = A_WIDTH + B_WIDTH + C_WIDTH + D_WIDTH
FF_DIM = ((8 * D_MODEL + 2) // 3 + 255) // 256 * 256

IN_SPLITS = (
    A_WIDTH, A_WIDTH, A_WIDTH,
    B_Q_RANK, B_KV_RANK, B_ROPE_DIM,
    C_WIDTH, C_WIDTH,
    D_WIDTH, D_WIDTH, D_WIDTH, D_WIDTH,
    2 * D_HEADS, 2 * D_HEADS,
    N_BRANCH * D_MODEL,
)
IN_DIM = sum(IN_SPLITS)

kernel_name = 'hybrid_parallel_bidir_encoder'


def rms_norm(x, g):
    xf = x.astype(jnp.float32)
    y = xf * lax.rsqrt(jnp.mean(jnp.square(xf), axis=-1, keepdims=True) + NORM_EPS)
    return (y * g.astype(jnp.float32)).astype(x.dtype)


def l2_norm(x):
    xf = x.astype(jnp.float32)
    return (xf * lax.rsqrt(jnp.sum(jnp.square(xf), axis=-1, keepdims=True) + NORM_EPS)).astype(x.dtype)


def rope_cos_sin(seq, dim):
    inv_freq = 1.0 / (ROPE_THETA ** (jnp.arange(0, dim, 2, dtype=jnp.float32) / dim))
    ang = jnp.arange(seq, dtype=jnp.float32)[:, None] * inv_freq[None, :]
    return jnp.cos(ang), jnp.sin(ang)


def apply_rope(x, cos, sin):
    xf = x.astype(jnp.float32)
    x1, x2 = jnp.split(xf, 2, axis=-1)
    c = cos[:, None, :]
    s = sin[:, None, :]
    return jnp.concatenate([x1 * c - x2 * s, x2 * c + x1 * s], axis=-1).astype(x.dtype)


def partial_rope(x, cos, sin, rope_dim):
    return jnp.concatenate([apply_rope(x[..., :rope_dim], cos, sin), x[..., rope_dim:]], axis=-1)


def centred_depthwise_conv(x, w):
    k_size = w.shape[0]
    left = k_size // 2
    right = k_size - 1 - left
    seq = x.shape[1]
    xp = jnp.pad(x, ((0, 0), (left, right), (0, 0)))
    out = xp[:, 0:seq] * w[0]
    for j in range(1, k_size):
        out = out + xp[:, j:j + seq] * w[j]
    return out


def dilated_attention(q, k, v):
    bsz, seq, heads, dh = q.shape
    scale = dh ** -0.5
    outs, lses = [], []
    for window, dil in A_PATTERNS:
        half = window // (2 * dil)
        n_side = -(-half // A_BLOCK)
        sub_len = seq // dil
        nb = -(-sub_len // A_BLOCK)
        lp = nb * A_BLOCK
        kw_len = (2 * n_side + 1) * A_BLOCK

        def to_sub(t):
            return t.reshape(bsz, sub_len, dil, heads, dh).transpose(0, 2, 1, 3, 4)

        qs = jnp.pad(to_sub(q), ((0, 0), (0, 0), (0, lp - sub_len), (0, 0), (0, 0)))
        qs = qs.reshape(bsz, dil, nb, A_BLOCK, heads, dh)
        kpad = ((0, 0), (0, 0), (n_side * A_BLOCK, lp - sub_len + n_side * A_BLOCK), (0, 0), (0, 0))
        kb = jnp.pad(to_sub(k), kpad).reshape(bsz, dil, nb + 2 * n_side, A_BLOCK, heads, dh)
        vb = jnp.pad(to_sub(v), kpad).reshape(bsz, dil, nb + 2 * n_side, A_BLOCK, heads, dh)
        kw = jnp.concatenate([kb[:, :, j:j + nb] for j in range(2 * n_side + 1)], axis=3)
        vw = jnp.concatenate([vb[:, :, j:j + nb] for j in range(2 * n_side + 1)], axis=3)
        qpos = jnp.arange(nb)[:, None] * A_BLOCK + jnp.arange(A_BLOCK)[None, :]
        kpos = jnp.arange(nb)[:, None] * A_BLOCK - n_side * A_BLOCK + jnp.arange(kw_len)[None, :]
        rel = kpos[:, None, :] - qpos[:, :, None]
        valid = (jnp.abs(rel) <= half) & (kpos[:, None, :] >= 0) & (kpos[:, None, :] < sub_len)
        s = jnp.einsum('brnqhd,brnkhd->brnhqk', qs, kw, preferred_element_type=jnp.float32) * scale
        s = jnp.where(valid[:, None], s, NEG_INF)
        m = jnp.max(s, axis=-1, keepdims=True)
        p = jnp.exp(s - m)
        z = jnp.sum(p, axis=-1, keepdims=True)
        o = jnp.einsum('brnhqk,brnkhd->brnqhd', (p / z).astype(v.dtype), vw)
        lse = (m + jnp.log(z))[..., 0]
        o = o.reshape(bsz, dil, lp, heads, dh)[:, :, :sub_len]
        o = o.transpose(0, 2, 1, 3, 4).reshape(bsz, seq, heads, dh)
        lse = lse.transpose(0, 1, 2, 4, 3).reshape(bsz, dil, lp, heads)[:, :, :sub_len]
        lse = lse.transpose(0, 2, 1, 3).reshape(bsz, seq, heads)
        outs.append(o)
        lses.append(lse)
    wts = jax.nn.softmax(jnp.stack(lses, axis=0), axis=0)
    out = jnp.sum(wts[..., None] * jnp.stack(outs, axis=0).astype(jnp.float32), axis=0)
    return out.astype(q.dtype)


def mla_attention(cq, ckv, kr, qa_g, wuq, kva_g, wukv, qn_g, kn_g):
    bsz, seq, _ = cq.shape
    q = (rms_norm(cq, qa_g) @ wuq).reshape(bsz, seq, B_HEADS, B_QK_DIM)
    kv = (rms_norm(ckv, kva_g) @ wukv).reshape(bsz, seq, B_HEADS, B_NOPE_DIM + B_V_DIM)
    k_nope, v = jnp.split(kv, [B_NOPE_DIM], axis=-1)
    k = jnp.concatenate([k_nope, jnp.broadcast_to(kr[:, :, None, :], (bsz, seq, B_HEADS, B_ROPE_DIM))], axis=-1)
    q = rms_norm(q, qn_g)
    k = rms_norm(k, kn_g)
    cos, sin = rope_cos_sin(seq, B_ROPE_DIM)
    q = jnp.concatenate([q[..., :B_NOPE_DIM], apply_rope(q[..., B_NOPE_DIM:], cos, sin)], axis=-1)
    k = jnp.concatenate([k[..., :B_NOPE_DIM], apply_rope(k[..., B_NOPE_DIM:], cos, sin)], axis=-1)
    scale = B_QK_DIM ** -0.5
    qb = q.reshape(bsz, seq // B_QBLOCK, B_QBLOCK, B_HEADS, B_QK_DIM).transpose(1, 0, 2, 3, 4)

    def attend(qi):
        s = jnp.einsum('bqhd,bkhd->bhqk', qi, k, preferred_element_type=jnp.float32) * scale
        p = jax.nn.softmax(s, axis=-1)
        return jnp.einsum('bhqk,bkhd->bqhd', p.astype(v.dtype), v)

    o = lax.map(attend, qb)
    return o.transpose(1, 0, 2, 3, 4).reshape(bsz, seq, B_WIDTH)


def linear_scan(a, b):
    def combine(e1, e2):
        a1, b1 = e1
        a2, b2 = e2
        return a1 * a2, a2 * b1 + b2
    _, h = lax.associative_scan(combine, (a, b), axis=1)
    return h


def rglru_branch(xb, gb, conv_w, conv_b, wr, br, wi, bi, lam):
    bsz, seq, width = xb.shape
    xc = centred_depthwise_conv(xb, conv_w) + conv_b
    xg = xc.reshape(bsz, seq, C_BLOCKS, C_BLOCK_DIM)
    xf = xc.astype(jnp.float32)

    def direction(d):
        r = jax.nn.sigmoid((jnp.einsum('bsgi,gio->bsgo', xg, wr[d]).reshape(bsz, seq, width) + br[d]).astype(jnp.float32))
        i = jax.nn.sigmoid((jnp.einsum('bsgi,gio->bsgo', xg, wi[d]).reshape(bsz, seq, width) + bi[d]).astype(jnp.float32))
        log_a = -C_GATE_C * jax.nn.softplus(-lam[d].astype(jnp.float32)) * r
        a = jnp.exp(log_a)
        u = jnp.sqrt(-jnp.expm1(2.0 * log_a)) * (i * xf)
        return a, u

    a_f, u_f = direction(0)
    h_fwd = linear_scan(a_f, u_f)
    a_b, u_b = direction(1)
    h_bwd = jnp.flip(linear_scan(jnp.flip(a_b, axis=1), jnp.flip(u_b, axis=1)), axis=1)
    return ((h_fwd + h_bwd) * jax.nn.gelu(gb.astype(jnp.float32))).astype(xb.dtype)


def chunk_gated_delta(q, k, v, g, beta):
    bsz, seq, heads, dk = q.shape
    dv = v.shape[-1]
    c = D_CHUNK
    nc = seq // c

    def chunks(t):
        return t.astype(jnp.float32).reshape(bsz, nc, c, heads, -1).transpose(1, 0, 3, 2, 4)

    qc = chunks(q) * (dk ** -0.5)
    kc = chunks(k)
    vc = chunks(v)
    bc = chunks(beta[..., None])
    gc = jnp.cumsum(chunks(g[..., None])[..., 0], axis=-1)
    idx = jnp.arange(c)
    incl = idx[:, None] >= idx[None, :]
    strict = idx[:, None] > idx[None, :]
    decay = jnp.exp(jnp.where(incl, gc[..., :, None] - gc[..., None, :], -jnp.inf))
    kk = jnp.einsum('nbhid,nbhjd->nbhij', kc * bc, kc)
    lmat = jnp.where(strict, kk * decay, 0.0)
    eye = jnp.eye(c, dtype=jnp.float32)
    t_inv = lax.linalg.triangular_solve(eye + lmat, jnp.broadcast_to(eye, lmat.shape),
                                        left_side=True, lower=True, unit_diagonal=True)
    u = t_inv @ (vc * bc)
    w = t_inv @ (kc * bc * jnp.exp(gc)[..., None])
    attn = jnp.where(incl, jnp.einsum('nbhid,nbhjd->nbhij', qc, kc) * decay, 0.0)

    def step(state, xs):
        q_i, k_i, u_i, w_i, g_i, a_i = xs
        v_new = u_i - w_i @ state
        o_i = (q_i * jnp.exp(g_i)[..., None]) @ state + a_i @ v_new
        g_last = g_i[..., -1:]
        state = state * jnp.exp(g_last)[..., None] + jnp.einsum(
            'bhck,bhcv->bhkv', k_i * jnp.exp(g_last - g_i)[..., None], v_new)
        return state, o_i

    state0 = jnp.zeros((bsz, heads, dk, dv), jnp.float32)
    _, o = lax.scan(step, state0, (qc, kc, u, w, gc, attn))
    return o.transpose(1, 0, 3, 2, 4).reshape(bsz, seq, heads, dv)


def gdn_branch(q, k, v, z, a_raw, b_raw, conv_w, a_log, dt_bias, on_g):
    bsz, seq, _ = q.shape
    out_dtype = z.dtype
    qkv = jax.nn.silu(centred_depthwise_conv(jnp.concatenate([q, k, v], axis=-1), conv_w))
    qh, kh, vh = jnp.split(qkv, 3, axis=-1)
    qh = l2_norm(qh.reshape(bsz, seq, D_HEADS, D_HEAD_DIM))
    kh = l2_norm(kh.reshape(bsz, seq, D_HEADS, D_HEAD_DIM))
    vh = vh.reshape(bsz, seq, D_HEADS, D_HEAD_DIM)
    a_raw = a_raw.reshape(bsz, seq, 2, D_HEADS).astype(jnp.float32)
    b_raw = b_raw.reshape(bsz, seq, 2, D_HEADS).astype(jnp.float32)
    g_f = -jnp.exp(a_log[0].astype(jnp.float32)) * jax.nn.softplus(a_raw[:, :, 0] + dt_bias[0].astype(jnp.float32))
    beta_f = jax.nn.sigmoid(b_raw[:, :, 0])
    g_b = -jnp.exp(a_log[1].astype(jnp.float32)) * jax.nn.softplus(a_raw[:, :, 1] + dt_bias[1].astype(jnp.float32))
    beta_b = jax.nn.sigmoid(b_raw[:, :, 1])
    o_fwd = chunk_gated_delta(qh, kh, vh, g_f, beta_f)
    o_bwd = jnp.flip(chunk_gated_delta(jnp.flip(qh, axis=1), jnp.flip(kh, axis=1), jnp.flip(vh, axis=1),
                                       jnp.flip(g_b, axis=1), jnp.flip(beta_b, axis=1)), axis=1)
    zh = z.reshape(bsz, seq, D_HEADS, D_HEAD_DIM).astype(jnp.float32)
    y = rms_norm(o_fwd + o_bwd, on_g) * jax.nn.silu(zh)
    return y.reshape(bsz, seq, D_WIDTH).astype(out_dtype)


def encoder_layer(x, ln1_g, w_in, a_qn_g, a_kn_g, b_qa_g, b_wuq, b_kva_g, b_wukv, b_qn_g, b_kn_g,
                  c_conv_w, c_conv_b, c_wr, c_br, c_wi, c_bi, c_lam, d_conv_w, d_a_log, d_dt_bias, d_on_g,
                  w_branch, w_out, ln2_g, w_up, w_down):
    bsz, seq, _ = x.shape
    xn = rms_norm(x, ln1_g)
    h = xn @ w_in
    split_at = [int(i) for i in np.cumsum(IN_SPLITS)[:-1]]
    (a_q, a_k, a_v, b_cq, b_ckv, b_kr, c_x, c_g,
     d_q, d_k, d_v, d_z, d_a, d_b, gates) = jnp.split(h, split_at, axis=-1)

    cos, sin = rope_cos_sin(seq, A_ROPE_DIM)
    aq = partial_rope(rms_norm(a_q.reshape(bsz, seq, A_HEADS, A_HEAD_DIM), a_qn_g), cos, sin, A_ROPE_DIM)
    ak = partial_rope(rms_norm(a_k.reshape(bsz, seq, A_HEADS, A_HEAD_DIM), a_kn_g), cos, sin, A_ROPE_DIM)
    o_a = dilated_attention(aq, ak, a_v.reshape(bsz, seq, A_HEADS, A_HEAD_DIM)).reshape(bsz, seq, A_WIDTH)
    o_b = mla_attention(b_cq, b_ckv, b_kr, b_qa_g, b_wuq, b_kva_g, b_wukv, b_qn_g, b_kn_g)
    o_c = rglru_branch(c_x, c_g, c_conv_w, c_conv_b, c_wr, c_br, c_wi, c_bi, c_lam)
    o_d = gdn_branch(d_q, d_k, d_v, d_z, d_a, d_b, d_conv_w, d_a_log, d_dt_bias, d_on_g)

    gate = jax.nn.sigmoid(gates.astype(jnp.float32)).reshape(bsz, seq, N_BRANCH, D_MODEL)
    merged = None
    start = 0
    for i, o_i in enumerate((o_a, o_b, o_c, o_d)):
        width = o_i.shape[-1]
        term = gate[:, :, i] * (o_i @ w_branch[start:start + width]).astype(jnp.float32)
        merged = term if merged is None else merged + term
        start += width
    x = x + merged.astype(x.dtype) @ w_out

    xn2 = rms_norm(x, ln2_g)
    g_ff, u_ff = jnp.split(xn2 @ w_up, 2, axis=-1)
    x = x + (jax.nn.silu(g_ff) * u_ff) @ w_down
    return x


def setup_inputs(seed: int = 0) -> dict:
    key = jax.random.key(seed)
    ks = jax.random.split(key, 32)
    f32 = jnp.float32
    nl = DEPTH

    def dense(k, shape, fan_in):
        return jax.random.normal(k, shape, f32) * (fan_in ** -0.5)

    def gain(k, shape):
        return 1.0 + 0.02 * jax.random.normal(k, shape, f32)

    def small(k, shape):
        return 0.02 * jax.random.normal(k, shape, f32)

    lam_u = jax.random.uniform(ks[20], (nl, 2, C_WIDTH), dtype=f32, minval=0.9, maxval=0.999)
    lam_p = lam_u ** (1.0 / C_GATE_C)
    c_lam = jnp.log(lam_p) - jnp.log1p(-lam_p)
    d_a_log = jnp.log(jax.random.uniform(ks[21], (nl, 2, D_HEADS), dtype=f32, minval=1.0, maxval=16.0))
    dt = jnp.exp(jax.random.uniform(ks[22], (nl, 2, D_HEADS), dtype=f32,
                                    minval=math.log(1e-3), maxval=math.log(1e-1)))
    d_dt_bias = dt + jnp.log(-jnp.expm1(-dt))

    return {
        'x_prompt': jax.random.normal(ks[0], (BATCH, SEQ, D_MODEL), f32),
        'x_sample': jax.random.normal(ks[1], (DEC_BATCH, DEC_SEQ, D_MODEL), f32),
        'ln1_g': gain(ks[2], (nl, D_MODEL)),
        'w_in': dense(ks[3], (nl, D_MODEL, IN_DIM), D_MODEL),
        'a_qn_g': gain(ks[4], (nl, A_HEAD_DIM)),
        'a_kn_g': gain(ks[5], (nl, A_HEAD_DIM)),
        'b_qa_g': gain(ks[6], (nl, B_Q_RANK)),
        'b_wuq': dense(ks[7], (nl, B_Q_RANK, B_HEADS * B_QK_DIM), B_Q_RANK),
        'b_kva_g': gain(ks[8], (nl, B_KV_RANK)),
        'b_wukv': dense(ks[9], (nl, B_KV_RANK, B_HEADS * (B_NOPE_DIM + B_V_DIM)), B_KV_RANK),
        'b_qn_g': gain(ks[10], (nl, B_QK_DIM)),
        'b_kn_g': gain(ks[11], (nl, B_QK_DIM)),
        'c_conv_w': dense(ks[12], (nl, C_CONV, C_WIDTH), C_CONV),
        'c_conv_b': small(ks[13], (nl, C_WIDTH)),
        'c_wr': dense(ks[14], (nl, 2, C_BLOCKS, C_BLOCK_DIM, C_BLOCK_DIM), C_BLOCK_DIM),
        'c_br': small(ks[15], (nl, 2, C_WIDTH)),
        'c_wi': dense(ks[16], (nl, 2, C_BLOCKS, C_BLOCK_DIM, C_BLOCK_DIM), C_BLOCK_DIM),
        'c_bi': small(ks[17], (nl, 2, C_WIDTH)),
        'c_lam': c_lam,
        'd_conv_w': dense(ks[18], (nl, D_CONV, 3 * D_WIDTH), D_CONV),
        'd_a_log': d_a_log,
        'd_dt_bias': d_dt_bias,
        'd_on_g': gain(ks[19], (nl, D_HEAD_DIM)),
        'w_branch': dense(ks[23], (nl, MIX_WIDTH, D_MODEL), MIX_WIDTH),
        'w_out': dense(ks[24], (nl, D_MODEL, D_MODEL), D_MODEL),
        'ln2_g': gain(ks[25], (nl, D_MODEL)),
        'w_up': dense(ks[26], (nl, D_MODEL, 2 * FF_DIM), D_MODEL),
        'w_down': dense(ks[27], (nl, FF_DIM, D_MODEL), FF_DIM),
    }


def reference(x_prompt, x_sample, ln1_g, w_in, a_qn_g, a_kn_g, b_qa_g, b_wuq, b_kva_g, b_wukv, b_qn_g, b_kn_g,
              c_conv_w, c_conv_b, c_wr, c_br, c_wi, c_bi, c_lam, d_conv_w, d_a_log, d_dt_bias, d_on_g,
              w_branch, w_out, ln2_g, w_up, w_down):
    def trunk(x):
        for l in range(DEPTH):
            x = encoder_layer(x, ln1_g[l], w_in[l], a_qn_g[l], a_kn_g[l], b_qa_g[l], b_wuq[l], b_kva_g[l],
                              b_wukv[l], b_qn_g[l], b_kn_g[l], c_conv_w[l], c_conv_b[l], c_wr[l], c_br[l],
                              c_wi[l], c_bi[l], c_lam[l], d_conv_w[l], d_a_log[l], d_dt_bias[l], d_on_g[l],
                              w_branch[l], w_out[l], ln2_g[l], w_up[l], w_down[l])
        return x

    y_prompt = trunk(x_prompt)
    y_sample = trunk(x_sample)
    return (y_prompt, y_sample)
```

```python
import numpy as np
import concourse.bass as bass
import concourse.mybir as mybir
from concourse.ap import AP
from concourse.bass_utils import run_bass_kernel_spmd
from contextlib import ExitStack

F32 = mybir.dt.float32; BF16 = mybir.dt.bfloat16
AF = mybir.ActivationFunctionType; ALU = mybir.AluOpType; AX = mybir.AxisListType
import os
_SK = set(os.environ.get('K_SKIP', '').split(','))
D = 1024; IN_DIM = 9136; FF = 2816; EPS = 1e-6
NEG = -30000.0

class Res:
    __slots__ = ("lw", "rd", "sem", "cnt", "name", "ex")
    def __init__(self, name="", ex=False):
        self.lw = None; self.rd = []; self.sem = None; self.cnt = 0; self.name = name; self.ex = ex

COMPUTE = ("pe", "act", "dve", "pool")
class Prog:
    def __init__(self, nc):
        self.nc = nc
        self.ops = {k: [] for k in ("pe", "act", "dve", "pool", "sp")}
        self.cnt = {k: 0 for k in COMPUTE}
        self.esem = {}
        self.waited = {k: {} for k in self.ops}
        self.dres = []
        self.nops = 0
        for k in COMPUTE:
            self.esem[k] = nc.alloc_semaphore(name="e_" + k)
    def eng(self, q):
        nc = self.nc
        return {"pe": nc.tensor, "act": nc.scalar, "dve": nc.vector, "pool": nc.gpsimd, "sp": nc.sync}[q]
    def dma_res(self, name):
        if not hasattr(self, "named"): self.named = {}
        if name in self.named: return self.named[name]
        r = Res(name); self.named[name] = r; r.sem = self.nc.alloc_semaphore(name="d_%s_%d" % (name, len(self.dres))); self.dres.append(r); return r
    def op(self, eng, fn, reads=(), writes=(), dma=None, q="sp"):
        deps = []
        for r in reads:
            if r.lw is not None: deps.append(r.lw)
            if r.ex: deps.extend(r.rd)
        for w in writes:
            if w.lw is not None: deps.append(w.lw)
            deps.extend(w.rd)
        if eng == "dma":
            queue = q
            dma.cnt += 16
            ev = (dma.sem, dma.cnt, "dma")
            inc = (dma.sem, 16)
        else:
            queue = eng
            self.cnt[eng] += 1
            ev = (self.esem[eng], self.cnt[eng], eng)
            inc = (self.esem[eng], 1)
        waits = []
        wd = self.waited[queue]
        for (sem, val, src) in deps:
            if src == queue and eng == "pe":
                continue
            key = id(sem)
            if wd.get(key, 0) >= val:
                continue
            wd[key] = val
            waits.append((sem, val))
        e = self.eng(queue)
        for sem, val in waits:
            e.wait_ge(sem, val)
        fn(e).then_inc(inc[0], inc[1])
        self.ops[queue].append(None)
        self.nops += 1
        for r in reads: r.rd.append(ev)
        for w in writes:
            w.lw = ev; w.rd = []
        return ev
    def barrier(self):
        evs = [(self.esem[k], self.cnt[k]) for k in COMPUTE if self.cnt[k] > 0]
        for r in self.dres:
            if r.cnt > 0: evs.append((r.sem, r.cnt))
        for qn in self.ops:
            wd = self.waited[qn]
            waits = []
            for sem, val in evs:
                if qn in COMPUTE and sem is self.esem[qn]: continue
                if wd.get(id(sem), 0) >= val: continue
                wd[id(sem)] = val; waits.append((sem, val))
            e = self.eng(qn)
            for sem, val in waits:
                e.wait_ge(sem, val)
    def emit(self, block):
        def run(e, lst):
            for fn, waits, inc in lst:
                for sem, val in waits:
                    e.wait_ge(sem, val)
                if fn is not None:
                    fn(e).then_inc(inc[0], inc[1])
        @block.tensor
        def _(e): run(e, self.ops["pe"])
        @block.scalar
        def _(e): run(e, self.ops["act"])
        @block.vector
        def _(e): run(e, self.ops["dve"])
        @block.gpsimd
        def _(e): run(e, self.ops["pool"])
        @block.sync
        def _(e): run(e, self.ops["sp"])

def rev_ap(ap2d):
    n = ap2d.shape[-1]
    a = ap2d.ap
    return AP(ap2d.tensor, ap2d.offset + (n - 1) * a[-1][0], [list(a[0]), [-a[-1][0], n]])

WSPECS = [("ln1_g", [2, 1024]), ("w_in", [2, 1024, 9136]), ("a_qn_g", [2, 64]), ("a_kn_g", [2, 64]),
          ("b_qa_g", [2, 256]), ("b_wuq", [2, 256, 768]), ("b_kva_g", [2, 128]), ("b_wukv", [2, 128, 1024]),
          ("b_qn_g", [2, 96]), ("b_kn_g", [2, 96]), ("c_conv_w", [2, 4, 512]), ("c_conv_b", [2, 512]),
          ("c_wr", [2, 2, 8, 64, 64]), ("c_br", [2, 2, 512]), ("c_wi", [2, 2, 8, 64, 64]), ("c_bi", [2, 2, 512]),
          ("c_lam", [2, 2, 512]), ("d_conv_w", [2, 4, 1536]), ("d_a_log", [2, 2, 4]), ("d_dt_bias", [2, 2, 4]),
          ("d_on_g", [2, 128]), ("w_branch", [2, 2048, 1024]), ("w_out", [2, 1024, 1024]), ("ln2_g", [2, 1024]),
          ("w_up", [2, 1024, 5632]), ("w_down", [2, 2816, 1024])]

def bc_mid(ap2d, n):
    a = ap2d.ap
    return AP(ap2d.tensor, ap2d.offset, [list(a[0]), [0, n], list(a[-1])])

def bc_last(ap2d, n):
    a = ap2d.ap
    return AP(ap2d.tensor, ap2d.offset, [list(a[0]), list(a[-1]), [0, n]])

def build(T, NL=2, debug=False, phases=None):
    HALF = T // 2; NG = T // 512; NT = T // 128; NCH = T // 64
    nc = bass.Bass("TRN2", target_bir_lowering=False)
    P = Prog(nc)
    es = ExitStack()
    uid = [0]
    def sbt(name, shape, dt=F32):
        uid[0] += 1
        return nc.sbuf_tensor('%s_%d' % (name, uid[0]), shape, dt)
    def din(name, shape, dt=F32): return nc.dram_tensor(name, list(shape), dt, kind="ExternalInput").ap()
    def dsc(name, shape, dt): return nc.dram_tensor(name, list(shape), dt, kind=("ExternalOutput" if (debug and name in debug) else "Internal")).ap()
    x_in = din("x", [T, D]); cfg_in = din("cfg", [128, 4]); ropeA = din("ropeA", [T, 16]); ropeB = din("ropeB", [T, 32])
    amask_in = din("amask", [20, 128, 512]); cst_in = din("cst", [128, 512])
    W = {n: din(n, s) for n, s in WSPECS}
    y_out = nc.dram_tensor("y", [T, D], F32, kind="ExternalOutput").ap()
    wb = {"w_in": dsc("w_in_bf", [2, 1024, IN_DIM], BF16), "b_wuq": dsc("wuq_bf", [2, 256, 768], BF16),
          "b_wukv": dsc("wukv_bf", [2, 128, 1024], BF16), "w_branch": dsc("wbr_bf", [2, 2048, 1024], BF16),
          "w_out": dsc("wout_bf", [2, 1024, 1024], BF16), "w_up": dsc("wup_bf", [2, 1024, 5632], BF16),
          "w_down": dsc("wdn_bf", [2, 2816, 1024], BF16)}
    R_wb = P.dma_res("wb")
    aqT = dsc("aqT", [4, 128, T], BF16); akT = dsc("akT", [4, 128, T], BF16); av = dsc("av", [T, 512], BF16)
    bqT = dsc("bqT", [8, 96, T], BF16); bkT = dsc("bkT", [8, 96, T], BF16); bv = dsc("bv", [T, 512], BF16)
    cxT = dsc("cxT", [1024, T], BF16)
    dqT = dsc("dqT", [2048, T], BF16)
    daT = dsc("daT", [8, T], F32); dbT = dsc("dbT", [8, T], F32)
    oT = dsc("oT", [2048, T], BF16)
    x1 = dsc("x1", [T, D], F32); x2 = dsc("x2", [T, D], F32)
    hfT = dsc("hfT", [512, T], F32)
    gcD = dsc("gcD", [16, T], F32)
    ofT = dsc("ofT", [512, T], F32)
    R = {n: P.dma_res(n) for n in ["aqT", "akT", "av", "bqT", "bkT", "bv", "cxT", "dqT", "daT", "dbT", "oT", "x1", "x2", "hfT", "gcD", "ofT", "y"]}

    def S(name, shape, dt=F32):
        return es.enter_context(sbt(name, list(shape), dt))
    def dma(out, in_, reads, writes, res, q="sp"):
        P.op("dma", lambda e: e.dma_start(out=out, in_=in_), reads=reads, writes=writes, dma=res, q=q)
    def load(tile_ap, dram_ap, res, src=()):
        dma(tile_ap, dram_ap, list(src), [res], res)
    def store(dram_ap, tile_ap, tres, dres):
        dma(dram_ap, tile_ap, [tres], [dres], dres, q="pool")
    def DVE(f, r=(), w=()): P.op("dve", f, r, w)
    def ACT(f, r=(), w=()): P.op("act", f, r, w)
    def PE(f, r=(), w=()): P.op("pe", f, r, w)
    def POOL(f, r=(), w=()): P.op("pool", f, r, w)

    with es:
        psf = [es.enter_context(nc.psum_tensor("psf%d" % i, [128, 512], F32)) for i in range(6)]
        psb = [es.enter_context(nc.psum_tensor("psb%d" % i, [128, 1024], BF16)) for i in range(2)]
        Rpsf = [Res(ex=True) for _ in range(6)]; Rpsb = [Res(ex=True) for _ in range(2)]
        pctr = [0, 0]
        def nps():
            i = pctr[0] % 6; pctr[0] += 1; return psf[i], Rpsf[i]
        def npsb():
            i = pctr[1] % 2; pctr[1] += 1; return psb[i], Rpsb[i]
        cst = S("cst", [128, 512]); Rcst = P.dma_res("cst")
        load(cst[:], cst_in[:, :], Rcst)
        cfg = S("cfg", [128, 4]); Rcfg = P.dma_res("cfg")
        load(cfg[:], cfg_in[:, :], Rcfg)
        identb = S("identb", [128, 128], BF16); Ridb = Res()
        DVE(lambda e: e.tensor_copy(out=identb[:], in_=cst[:, 0:128]), [Rcst], [Ridb])
        onesb = S("onesb", [128, 128], BF16); Rones = Res()
        DVE(lambda e: e.memset(onesb[:], 1.0), [], [Rones])
        ident = cst[:, 0:128]
        CONSTS = [Rcst, Rcfg, Ridb, Rones]

        def phase0():
            with ExitStack() as ps:
                tin = [ps.enter_context(sbt("cv_in%d" % i, [128, 2048], F32)) for i in range(2)]
                tout = [ps.enter_context(sbt("cv_out%d" % i, [128, 2048], BF16)) for i in range(2)]
                Rin = [P.dma_res("cvi%d" % i) for i in range(2)]; Rout = [Res(), Res()]
                k = 0
                for name, dst in wb.items():
                    src = W[name]
                    rows, cols = src.shape[1], src.shape[2]
                    for l in range(NL):
                        for r0 in range(0, rows, 128):
                            for c0 in range(0, cols, 2048):
                                cw = min(2048, cols - c0); b = k % 2; k += 1
                                load(tin[b][:, 0:cw], src[l, r0:r0 + 128, c0:c0 + cw], Rin[b])
                                if b == 0:
                                    DVE(lambda e, b=b, cw=cw: e.tensor_copy(out=tout[b][:, 0:cw], in_=tin[b][:, 0:cw]), [Rin[b]], [Rout[b]])
                                else:
                                    ACT(lambda e, b=b, cw=cw: e.activation(out=tout[b][:, 0:cw], in_=tin[b][:, 0:cw], func=AF.Copy), [Rin[b]], [Rout[b]])
                                store(dst[l, r0:r0 + 128, c0:c0 + cw], tout[b][:, 0:cw], Rout[b], R_wb)
            P.barrier()

        def norm_T(ph, src_ap, t0, ntok, gt, Rg, xt, Rxt, xnb, Rxnb, xnT, RxnT, junk, Rjunk, st, Rst, srcres):
            nj = ntok // 128
            load(xt[:, 0:nj, :], src_ap[t0:t0 + ntok, :].rearrange("(j p) d -> p j d", p=128), Rxt, src=srcres)
            for j in range(nj):
                ACT(lambda e, j=j: e.activation(out=junk[:], in_=xt[:, j, :], func=AF.Square), [Rxt], [Rjunk])
                DVE(lambda e, j=j: e.reduce_sum(out=st[:, j:j + 1], in_=junk[:], axis=AX.X), [Rjunk], [Rst])
            ACT(lambda e: e.activation(out=st[:, 8:8 + nj], in_=st[:, 0:nj], func=AF.Sqrt, scale=1.0 / D, bias=EPSC[:, 0:1]), [Rst, REPS], [Rst])
            DVE(lambda e: e.reciprocal(out=st[:, 16:16 + nj], in_=st[:, 8:8 + nj]), [Rst], [Rst])
            for j in range(nj):
                DVE(lambda e, j=j: e.scalar_tensor_tensor(out=xnb[:, j, :], in0=xt[:, j, :], scalar=st[:, 16 + j:17 + j], in1=gt[:],
                                                          op0=ALU.mult, op1=ALU.mult), [Rxt, Rst, Rg], [Rxnb])
            for kc in range(8):
                pb, Rpb = npsb()
                for j in range(nj):
                    PE(lambda e, j=j, kc=kc, pb=pb: e.transpose(out=pb[:, j * 128:(j + 1) * 128], in_=xnb[:, j, kc * 128:(kc + 1) * 128], identity=identb[:]),
                       [Rxnb, Ridb], [Rpb])
                if kc % 2 == 0:
                    DVE(lambda e, kc=kc, pb=pb: e.tensor_copy(out=xnT[:, kc, 0:ntok], in_=pb[:, 0:ntok]), [Rpb], [RxnT])
                else:
                    ACT(lambda e, kc=kc, pb=pb: e.activation(out=xnT[:, kc, 0:ntok], in_=pb[:, 0:ntok], func=AF.Copy), [Rpb], [RxnT])

        EPSC = S("epsc", [128, 1]); REPS = Res()
        DVE(lambda e: e.memset(EPSC[:], EPS), [], [REPS])

        def bcast_rows(dst_tile, src_row_ap, reps, width, res):
            src = src_row_ap.rearrange("(o w) -> o w", o=1).broadcast_to([128, width])
            for r_ in range(reps):
                load(dst_tile[:, r_ * width:(r_ + 1) * width], src, res)

        def phaseA1(l, xsrc, xres):
            with ExitStack() as ps:
                T_ = lambda n, s, d=F32: ps.enter_context(sbt(n, list(s), d))
                NC1 = 3088
                wA = T_("wA1", [128, 8, NC1], BF16); RwA = P.dma_res("wA1")
                for kc in range(8):
                    load(wA[:, kc, :], wb["w_in"][l, kc * 128:(kc + 1) * 128, 1952:5040], RwA, src=[R_wb])
                gt = T_("gt", [128, D]); Rg = P.dma_res("gtA1")
                load(gt[:], W["ln1_g"][l:l + 1, :].broadcast_to([128, D]), Rg)
                xt = T_("xt", [128, 4, D]); Rxt = P.dma_res("xtA1")
                xnb = T_("xnb", [128, 4, D], BF16); Rxnb = Res()
                xnT = T_("xnT", [128, 8, 512], BF16); RxnT = Res()
                junk = T_("junk", [128, D]); Rjunk = Res(); st = T_("st", [128, 24]); Rst = Res()
                stg = [T_("stgA1_%d" % i, [128, 24, 512], BF16) for i in range(2)]; Rstg = [Res(), Res()]
                sab = [T_("sab%d" % i, [8, 2, 512]) for i in range(2)]; Rsab = [Res(), Res()]
                for g in range(NG):
                    t0 = g * 512; b = g % 2
                    norm_T("A1", xsrc, t0, 512, gt, Rg, xt, Rxt, xnb, Rxnb, xnT, RxnT, junk, Rjunk, st, Rst, [xres])
                    for c in range(24):
                        pt, Rp = nps()
                        for kc in range(8):
                            PE(lambda e, c=c, kc=kc, pt=pt: e.matmul(pt[:, :], wA[:, kc, c * 128:(c + 1) * 128], xnT[:, kc, :], start=(kc == 0), stop=(kc == 7)),
                               [RwA, RxnT], [Rp])
                        if c % 2 == 0:
                            DVE(lambda e, c=c, pt=pt, b=b: e.tensor_copy(out=stg[b][:, c, :], in_=pt[:, :]), [Rp], [Rstg[b]])
                        else:
                            ACT(lambda e, c=c, pt=pt, b=b: e.activation(out=stg[b][:, c, :], in_=pt[:, :], func=AF.Copy), [Rp], [Rstg[b]])
                    for i2 in range(2):
                        pt, Rp = nps()
                        for kc in range(8):
                            PE(lambda e, kc=kc, pt=pt, i2=i2: e.matmul(pt[0:8, :], wA[:, kc, 3072 + 8 * i2:3080 + 8 * i2], xnT[:, kc, :], start=(kc == 0), stop=(kc == 7)),
                               [RwA, RxnT], [Rp])
                        DVE(lambda e, pt=pt, i2=i2, b=b: e.tensor_copy(out=sab[b][:, i2, :], in_=pt[0:8, :]), [Rp], [Rsab[b]])
                    store(cxT[:, t0:t0 + 512].rearrange("(c p) t -> p c t", p=128), stg[b][:, 0:8, :], Rstg[b], R["cxT"])
                    store(dqT[:, t0:t0 + 512].rearrange("(c p) t -> p c t", p=128), stg[b][:, 8:24, :], Rstg[b], R["dqT"])
                    store(daT[:, t0:t0 + 512], sab[b][:, 0, :], Rsab[b], R["daT"])
                    store(dbT[:, t0:t0 + 512], sab[b][:, 1, :], Rsab[b], R["dbT"])
            P.barrier()

        def phaseA2(l, xsrc, xres):
            with ExitStack() as ps:
                T_ = lambda n, s, d=F32: ps.enter_context(sbt(n, list(s), d))
                wA = T_("wA2", [128, 8, 1952], BF16); RwA = P.dma_res("wA2")
                for kc in range(8):
                    load(wA[:, kc, :], wb["w_in"][l, kc * 128:(kc + 1) * 128, 0:1952], RwA, src=[R_wb])
                wuq = T_("wuq", [128, 2, 768], BF16); wukv = T_("wukv", [128, 1024], BF16)
                load(wuq[:], wb["b_wuq"][l].rearrange("(k p) n -> p k n", p=128), RwA, src=[R_wb])
                load(wukv[:], wb["b_wukv"][l], RwA, src=[R_wb])
                gt = T_("gt", [128, D]); Rg = P.dma_res("gtA2")
                load(gt[:], W["ln1_g"][l:l + 1, :].broadcast_to([128, D]), Rg)
                gAq = T_("gAq", [128, 512]); gAk = T_("gAk", [128, 512]); gBq = T_("gBq", [128, 768]); gBk = T_("gBk", [128, 768])
                gqa = T_("gqa", [128, 256]); gkva = T_("gkva", [128, 128])
                bcast_rows(gAq, W["a_qn_g"][l], 8, 64, Rg); bcast_rows(gAk, W["a_kn_g"][l], 8, 64, Rg)
                bcast_rows(gBq, W["b_qn_g"][l], 8, 96, Rg); bcast_rows(gBk, W["b_kn_g"][l], 8, 96, Rg)
                bcast_rows(gqa, W["b_qa_g"][l], 1, 256, Rg); bcast_rows(gkva, W["b_kva_g"][l], 1, 128, Rg)
                xt = T_("xt", [128, 4, D]); Rxt = P.dma_res("xtA2")
                xnb = T_("xnb", [128, 4, D], BF16); Rxnb = Res()
                xnT = T_("xnT", [128, 8, 512], BF16); RxnT = Res()
                junk = T_("junk", [128, D]); Rjunk = Res(); st = T_("st", [128, 24]); Rst = Res()
                rA = T_("rA", [128, 4, 16]); rB = T_("rB", [128, 4, 32]); Rrope = P.dma_res("rope")
                sAq = T_("sAq", [128, 4, 512], BF16); sAk = T_("sAk", [128, 4, 512], BF16); sAv = T_("sAv", [128, 4, 512], BF16)
                sBq = T_("sBq", [128, 8, 512], BF16); sBk = T_("sBk", [128, 8, 512], BF16); sBv = T_("sBv", [128, 4, 512], BF16)
                RsAq, RsAk, RsAv, RsBq, RsBk, RsBv = [Res() for _ in range(6)]
                hsq = T_("hsq", [128, 768]); Rhsq = Res(); hn = T_("hn", [128, 768]); Rhn = Res()
                hst = T_("hst", [128, 24]); Rhst = Res(); tr = T_("tr", [128, 4, 128]); Rtr = Res()
                ob = T_("ob", [128, 768], BF16); Rob = Res()
                qf = T_("qf", [128, 768]); Rqf = Res(); kf = T_("kf", [128, 768]); Rkf = Res()
                krs = T_("krs", [128, 32]); Rkrs = Res()
                cqb = T_("cqb", [128, 384], BF16); Rcqb = Res(); cT = T_("cT", [128, 3, 128], BF16); RcT = Res()

                def head_proc(src3, rsrc, H, Dh, gain, r0, nf, cos, sin):
                    HD = H * Dh
                    v3 = lambda t: t[:, 0:HD].rearrange("p (h d) -> p h d", d=Dh)
                    ACT(lambda e: e.activation(out=v3(hsq), in_=src3, func=AF.Square), rsrc, [Rhsq])
                    DVE(lambda e: e.reduce_sum(out=hst[:, 0:H], in_=v3(hsq), axis=AX.X), [Rhsq], [Rhst])
                    ACT(lambda e: e.activation(out=hst[:, 8:8 + H], in_=hst[:, 0:H], func=AF.Sqrt, scale=1.0 / Dh, bias=EPSC[:, 0:1]), [Rhst, REPS], [Rhst])
                    DVE(lambda e: e.reciprocal(out=hst[:, 16:16 + H], in_=hst[:, 8:8 + H]), [Rhst], [Rhst])
                    DVE(lambda e: e.tensor_tensor(out=v3(hn), in0=src3, in1=bc_last(hst[:, 16:16 + H], Dh), op=ALU.mult), rsrc + [Rhst], [Rhn])
                    DVE(lambda e: e.tensor_tensor(out=hn[:, 0:HD], in0=hn[:, 0:HD], in1=gain[:, 0:HD], op=ALU.mult), [Rhn, Rg], [Rhn])
                    ACT(lambda e: e.activation(out=ob[:, 0:HD], in_=hn[:, 0:HD], func=AF.Copy), [Rhn], [Rob])
                    a = v3(hn)[:, :, r0:r0 + nf]; b_ = v3(hn)[:, :, r0 + nf:r0 + 2 * nf]
                    c = bc_mid(cos, H); s = bc_mid(sin, H)
                    tv = lambda i: tr[:, i, 0:H * nf].rearrange("p (h f) -> p h f", f=nf)
                    DVE(lambda e: e.tensor_tensor(out=tv(0), in0=a, in1=c, op=ALU.mult), [Rhn, Rrope], [Rtr])
                    DVE(lambda e: e.tensor_tensor(out=tv(1), in0=b_, in1=s, op=ALU.mult), [Rhn, Rrope], [Rtr])
                    DVE(lambda e: e.tensor_tensor(out=tv(2), in0=b_, in1=c, op=ALU.mult), [Rhn, Rrope], [Rtr])
                    DVE(lambda e: e.tensor_tensor(out=tv(3), in0=a, in1=s, op=ALU.mult), [Rhn, Rrope], [Rtr])
                    DVE(lambda e: e.tensor_tensor(out=v3(ob)[:, :, r0:r0 + nf], in0=tv(0), in1=tv(1), op=ALU.subtract), [Rtr], [Rob])
                    DVE(lambda e: e.tensor_tensor(out=v3(ob)[:, :, r0 + nf:r0 + 2 * nf], in0=tv(2), in1=tv(3), op=ALU.add), [Rtr], [Rob])

                def rms_rows(src, rsrc, n, gain, dst, col0):
                    ACT(lambda e: e.activation(out=hsq[:, 0:n], in_=src, func=AF.Square), rsrc, [Rhsq])
                    DVE(lambda e: e.reduce_sum(out=hst[:, 0:1], in_=hsq[:, 0:n], axis=AX.X), [Rhsq], [Rhst])
                    ACT(lambda e: e.activation(out=hst[:, 8:9], in_=hst[:, 0:1], func=AF.Sqrt, scale=1.0 / n, bias=EPSC[:, 0:1]), [Rhst, REPS], [Rhst])
                    DVE(lambda e: e.reciprocal(out=hst[:, 16:17], in_=hst[:, 8:9]), [Rhst], [Rhst])
                    DVE(lambda e: e.scalar_tensor_tensor(out=dst[:, col0:col0 + n], in0=src, scalar=hst[:, 16:17], in1=gain[:, 0:n], op0=ALU.mult, op1=ALU.mult),
                        rsrc + [Rhst, Rg], [Rcqb])

                for g in range(NG):
                    t0 = g * 512
                    norm_T("A2", xsrc, t0, 512, gt, Rg, xt, Rxt, xnb, Rxnb, xnT, RxnT, junk, Rjunk, st, Rst, [xres])
                    load(rA[:], ropeA[t0:t0 + 512, :].rearrange("(j p) f -> p j f", p=128), Rrope)
                    load(rB[:], ropeB[t0:t0 + 512, :].rearrange("(j p) f -> p j f", p=128), Rrope)
                    for j in range(4):
                        tsl = slice(j * 128, (j + 1) * 128)
                        def proj(c0, n):
                            pt, Rp = nps()
                            for kc in range(8):
                                PE(lambda e, kc=kc, pt=pt: e.matmul(pt[:, 0:n], xnT[:, kc, tsl], wA[:, kc, c0:c0 + n], start=(kc == 0), stop=(kc == 7)), [RwA, RxnT], [Rp])
                            return pt, Rp
                        for (c0, gain, sdst, Rs) in (((0, gAq, sAq, RsAq), (512, gAk, sAk, RsAk)) if 'Aqk' not in _SK else ()):
                            pt, Rp = proj(c0, 512)
                            head_proc(pt[:, :].rearrange("p (h d) -> p h d", d=64), [Rp], 8, 64, gain, 0, 8, rA[:, j, 0:8], rA[:, j, 8:16])
                            pb, Rpb = npsb()
                            for pr in range(4):
                                PE(lambda e, pr=pr, pb=pb: e.transpose(out=pb[:, pr * 128:(pr + 1) * 128], in_=ob[:, pr * 128:(pr + 1) * 128], identity=identb[:]), [Rob, Ridb], [Rpb])
                            ACT(lambda e, pb=pb, sdst=sdst: e.activation(out=sdst[:, :, tsl], in_=pb[:, 0:512].rearrange("p (a t) -> p a t", t=128), func=AF.Copy), [Rpb], [Rs])
                        pt, Rp = proj(1024, 512)
                        ACT(lambda e, pt=pt: e.activation(out=sAv[:, j, :], in_=pt[:, :], func=AF.Copy), [Rp], [RsAv])
                        if 'B' in _SK: continue
                        ptB, RpB = proj(1536, 416)
                        rms_rows(ptB[:, 0:256], [RpB], 256, gqa, cqb, 0)
                        rms_rows(ptB[:, 256:384], [RpB], 128, gkva, cqb, 256)
                        pb, Rpb = npsb()
                        for i3 in range(3):
                            PE(lambda e, i3=i3, pb=pb: e.transpose(out=pb[:, i3 * 128:(i3 + 1) * 128], in_=cqb[:, i3 * 128:(i3 + 1) * 128], identity=identb[:]), [Rcqb, Ridb], [Rpb])
                        DVE(lambda e, pb=pb: e.tensor_copy(out=cT[:], in_=pb[:, 0:384].rearrange("p (a t) -> p a t", t=128)), [Rpb], [RcT])
                        if 'Bq' in _SK: continue
                        for (c0, n) in ((0, 512), (512, 256)):
                            pt, Rp = nps()
                            for kc in range(2):
                                PE(lambda e, kc=kc, pt=pt, c0=c0, n=n: e.matmul(pt[:, 0:n], cT[:, kc, :], wuq[:, kc, c0:c0 + n], start=(kc == 0), stop=(kc == 1)), [RcT, RwA], [Rp])
                            DVE(lambda e, pt=pt, c0=c0, n=n: e.tensor_copy(out=qf[:, c0:c0 + n], in_=pt[:, 0:n]), [Rp], [Rqf])
                        head_proc(qf[:, :].rearrange("p (h d) -> p h d", d=96), [Rqf], 8, 96, gBq, 64, 16, rB[:, j, 0:16], rB[:, j, 16:32])
                        pb, Rpb = npsb()
                        for h in range(8):
                            PE(lambda e, h=h, pb=pb: e.transpose(out=pb[0:96, h * 128:(h + 1) * 128], in_=ob[:, h * 96:(h + 1) * 96], identity=identb[:]), [Rob, Ridb], [Rpb])
                        ACT(lambda e, pb=pb: e.activation(out=sBq[0:96, :, tsl], in_=pb[0:96, :].rearrange("p (a t) -> p a t", t=128), func=AF.Copy), [Rpb], [RsBq])
                        if 'Bkv' in _SK: continue
                        kf3 = kf[:, :].rearrange("p (h d) -> p h d", d=96)
                        for half in range(2):
                            pt, Rp = nps()
                            PE(lambda e, pt=pt, half=half: e.matmul(pt[:, :], cT[:, 2, :], wukv[:, half * 512:(half + 1) * 512], start=True, stop=True), [RcT, RwA], [Rp])
                            kv3 = pt[:, :].rearrange("p (h d) -> p h d", d=128)
                            DVE(lambda e, kv3=kv3, half=half: e.tensor_copy(out=kf3[:, half * 4:(half + 1) * 4, 0:64], in_=kv3[:, :, 0:64]), [Rp], [Rkf])
                            ACT(lambda e, kv3=kv3, half=half: e.activation(out=sBv[:, j, half * 256:(half + 1) * 256].rearrange("p (h d) -> p h d", d=64), in_=kv3[:, :, 64:128], func=AF.Copy), [Rp], [RsBv])
                        DVE(lambda e, ptB=ptB: e.tensor_copy(out=krs[:], in_=ptB[:, 384:416]), [RpB], [Rkrs])
                        DVE(lambda e: e.tensor_copy(out=kf3[:, :, 64:96], in_=bc_mid(krs[:, :], 8)), [Rkrs], [Rkf])
                        head_proc(kf3, [Rkf], 8, 96, gBk, 64, 16, rB[:, j, 0:16], rB[:, j, 16:32])
                        pb, Rpb = npsb()
                        for h in range(8):
                            PE(lambda e, h=h, pb=pb: e.transpose(out=pb[0:96, h * 128:(h + 1) * 128], in_=ob[:, h * 96:(h + 1) * 96], identity=identb[:]), [Rob, Ridb], [Rpb])
                        ACT(lambda e, pb=pb: e.activation(out=sBk[0:96, :, tsl], in_=pb[0:96, :].rearrange("p (a t) -> p a t", t=128), func=AF.Copy), [Rpb], [RsBk])
                    if 'st' in _SK: continue
                    store(aqT[:, :, t0:t0 + 512].rearrange("a p t -> p a t"), sAq[:], RsAq, R["aqT"])
                    store(akT[:, :, t0:t0 + 512].rearrange("a p t -> p a t"), sAk[:], RsAk, R["akT"])
                    store(av[t0:t0 + 512, :].rearrange("(j p) c -> p j c", p=128), sAv[:], RsAv, R["av"])
                    store(bqT[:, :, t0:t0 + 512].rearrange("h p t -> p h t"), sBq[0:96, :, :], RsBq, R["bqT"])
                    store(bkT[:, :, t0:t0 + 512].rearrange("h p t -> p h t"), sBk[0:96, :, :], RsBk, R["bkT"])
                    store(bv[t0:t0 + 512, :].rearrange("(j p) c -> p j c", p=128), sBv[:], RsBv, R["bv"])
            P.barrier()

        def attention(tag, qsrc, ksrc, vsrc, Rq, Rk, Rv, dk, scale, orow0, windowed):
            with ExitStack() as ps:
                T_ = lambda n, s, d=F32: ps.enter_context(sbt(n, list(s), d))
                KT = T_("KT", [128, T], BF16); RKT = P.dma_res("KT" + tag)
                Vt = T_("Vt", [128, NT, 128], BF16); RVt = P.dma_res("Vt" + tag)
                DVE(lambda e: e.memset(Vt[:], 1.0), [], [RVt])
                QT = [T_("QT%d" % i, [128, 512], BF16) for i in range(2)]; RQT = [P.dma_res("QT%d%s" % (i, tag)) for i in range(2)]
                pT = [T_("pT%d" % i, [128, 512], BF16) for i in range(6)]; RpT = [Res() for _ in range(6)]
                ev = [T_("ev%d" % i, [128, 512]) for i in range(2)]; Rev = [Res(), Res()]; rcp = T_("rcp", [64, 512]); Rrcp = Res()
                ostg = [T_("ostg%d" % i, [64, 512], BF16) for i in range(2)]; Rostg = [Res(), Res()]
                if windowed:
                    am32 = T_("am32", [128, 512]); Ram32 = P.dma_res("am32")
                    amk = T_("amk", [128, 20, 512], BF16); Ramk = Res()
                    for m in range(20):
                        load(am32[:], amask_in[m], Ram32)
                        DVE(lambda e, m=m: e.tensor_copy(out=amk[:, m, :], in_=am32[:]), [Ram32], [Ramk])
                LA = 3
                it = [0]
                def ring():
                    si = it[0] % 4; it[0] += 1; return si
                pTi = [0]
                for h in range(8):
                    load(KT[0:dk, :], ksrc(h), RKT, src=[Rk])
                    vsr = vsrc(h).rearrange("(n p) c -> p n c", p=128)
                    for n0_ in range(0, NT, 16):
                        load(Vt[:, n0_:n0_ + 16, 0:64], vsr[:, n0_:n0_ + 16, :], RVt, src=[Rv])
                    items = []
                    for g in range(NG):
                        q0 = g * 512
                        if windowed:
                            k_lo = max(0, (q0 - 1024) // 128); k_hi = min(NT, (q0 + 512 + 1024) // 128)
                        else:
                            k_lo, k_hi = 0, NT
                        for kt in range(k_lo, k_hi):
                            items.append((g, kt, kt == k_lo, kt == k_hi - 1))
                    n_it = len(items)
                    load(QT[0][0:dk, :], qsrc(h)[:, 0:512], RQT[0], src=[Rq])
                    slot = {}
                    pend = []
                    for step in range(n_it + LA + 3):
                        if step < n_it:
                            g, kt, first, last = items[step]
                            q0 = g * 512; qb = g % 2
                            if first and g + 1 < NG:
                                load(QT[1 - qb][0:dk, :], qsrc(h)[:, q0 + 512:q0 + 1024], RQT[1 - qb], src=[Rq])
                            si = ring(); pi = pTi[0] % 6; pTi[0] += 1
                            slot[step] = pi
                            pS, RpS = psf[si], Rpsf[si]
                            PE(lambda e, pS=pS, kt=kt, qb=qb: e.matmul(pS[:, :], KT[0:dk, kt * 128:(kt + 1) * 128], QT[qb][0:dk, :], start=True, stop=True),
                               [RKT, RQT[qb]], [RpS])
                            cross = ((kt * 128) // HALF) != (q0 // HALF)
                            bcol = cfg[:, 1:2] if cross else cfg[:, 0:1]
                            ACT(lambda e, pS=pS, pi=pi, bcol=bcol: e.activation(out=pT[pi][:], in_=pS[:, :], func=AF.Exp, bias=bcol, scale=scale), [RpS, Rcfg], [RpT[pi]])
                            if windowed:
                                m = (kt * 128 - q0 + 1024) // 128
                                DVE(lambda e, pi=pi, m=m: e.tensor_tensor(out=pT[pi][:], in0=pT[pi][:], in1=amk[:, m, :], op=ALU.mult), [RpT[pi], Ramk], [RpT[pi]])
                        j = step - LA
                        if 0 <= j < n_it:
                            g, kt, first, last = items[j]
                            q0 = g * 512; qb = g % 2; pi = slot.pop(j)
                            po, Rpo = psf[4 + qb], Rpsf[4 + qb]
                            PE(lambda e, po=po, kt=kt, pi=pi, first=first, last=last: e.matmul(po[:, :], Vt[:, kt, :], pT[pi][:], start=first, stop=last),
                               [RVt, RpT[pi]], [Rpo])
                            if last:
                                DVE(lambda e, po=po, qb=qb: e.tensor_copy(out=ev[qb][:], in_=po[:, :]), [Rpo], [Rev[qb]])
                                def fin(qb=qb, q0=q0):
                                    si = ring()
                                    pd, Rpd = psf[si], Rpsf[si]
                                    PE(lambda e, pd=pd: e.matmul(pd[0:64, :], cst[:, 448:512], ev[qb][:], start=True, stop=True), [Rev[qb], Rcst], [Rpd])
                                    DVE(lambda e, pd=pd: e.reciprocal(out=rcp[:], in_=pd[0:64, :]), [Rpd], [Rrcp])
                                    DVE(lambda e: e.tensor_tensor(out=ostg[qb][:], in0=ev[qb][0:64, :], in1=rcp[:], op=ALU.mult), [Rev[qb], Rrcp], [Rostg[qb]])
                                    store(oT[orow0 + h * 64:orow0 + (h + 1) * 64, q0:q0 + 512], ostg[qb][:], Rostg[qb], R["oT"])
                                pend.append((step + 3, fin))
                        while pend and pend[0][0] <= step:
                            pend.pop(0)[1]()
                    assert not pend
            P.barrier()

        def phaseC_rglru(l):
            SEG = T // 4; NB = SEG // 512
            with ExitStack() as ps:
                T_ = lambda n, s, d=F32: ps.enter_context(sbt(n, list(s), d))
                cols = T_("rcols", [128, 4, 16]); Rcols = P.dma_res("rcols")
                colap = lambda a: a.rearrange("(p o) -> p o", o=1)
                for c in range(4):
                    cs = slice(c * 128, (c + 1) * 128)
                    for j in range(4):
                        load(cols[:, c, j:j + 1], colap(W["c_conv_w"][l, j, cs]), Rcols)
                    load(cols[:, c, 4:5], colap(W["c_conv_b"][l, cs]), Rcols)
                    for d in range(2):
                        load(cols[:, c, 5 + d:6 + d], colap(W["c_br"][l, d, cs]), Rcols)
                        load(cols[:, c, 7 + d:8 + d], colap(W["c_bi"][l, d, cs]), Rcols)
                        load(cols[:, c, 9 + d:10 + d], colap(W["c_lam"][l, d, cs]), Rcols)
                ACT(lambda e: e.activation(out=cols[:, :, 11:13], in_=cols[:, :, 9:11], func=AF.Exp, scale=-1.0), [Rcols], [Rcols])
                ACT(lambda e: e.activation(out=cols[:, :, 11:13], in_=cols[:, :, 11:13], func=AF.Ln, bias=1.0), [Rcols], [Rcols])
                DVE(lambda e: e.tensor_scalar(out=cols[:, :, 13:15], in0=cols[:, :, 11:13], scalar1=-16.0, scalar2=None, op0=ALU.mult), [Rcols], [Rcols])
                DVE(lambda e: e.tensor_scalar(out=cols[:, :, 11:13], in0=cols[:, :, 11:13], scalar1=-8.0, scalar2=None, op0=ALU.mult), [Rcols], [Rcols])
                w32 = T_("w32", [128, 16, 128]); Rw32 = P.dma_res("w32"); wbd = T_("wbd", [128, 16, 128], BF16); Rwbd = Res()
                DVE(lambda e: e.memset(w32[:], 0.0), [], [Rw32])
                for d in range(2):
                    for gi_, nm in enumerate(("c_wr", "c_wi")):
                        for c in range(4):
                            idx = (d * 2 + gi_) * 4 + c
                            load(w32[0:64, idx, 0:64], W[nm][l, d, 2 * c], Rw32)
                            load(w32[64:128, idx, 64:128], W[nm][l, d, 2 * c + 1], Rw32)
                DVE(lambda e: e.tensor_copy(out=wbd[:], in_=w32[:]), [Rw32], [Rwbd])
                xin = T_("xin", [128, SEG + 4], BF16); Rxin = P.dma_res("xin")
                xc = T_("xc", [128, SEG]); Rxc = Res(); xcb = T_("xcb", [128, SEG], BF16); Rxcb = Res()
                tA = T_("tA", [128, SEG]); RtA = P.dma_res("tA"); tB = T_("tB", [128, SEG]); RtB = Res(); tC = T_("tC", [128, SEG]); RtC = Res()
                cg = T_("cg", [128, SEG], BF16); Rcg = P.dma_res("cg"); ot = T_("ot", [128, SEG], BF16); Rot = Res()
                car = T_("car", [128, 8]); Rcar = Res()
                def seg_common(c, s, d):
                    t0 = s * SEG; r0 = c * 128
                    DVE(lambda e: e.memset(xin[:], 0.0), [], [Rxin])
                    lo = max(0, t0 - 2); hi = min(T, t0 + SEG + 1)
                    load(xin[:, 2 - (t0 - lo):2 + (hi - t0)], cxT[r0:r0 + 128, lo:hi], Rxin, src=[R["cxT"]])
                    if s == 2:
                        DVE(lambda e: e.tensor_scalar(out=xin[:, 0:2], in0=xin[:, 0:2], scalar1=cfg[:, 2:3], scalar2=None, op0=ALU.mult), [Rxin, Rcfg], [Rxin])
                    if s == 1:
                        DVE(lambda e: e.tensor_scalar(out=xin[:, SEG + 2:SEG + 3], in0=xin[:, SEG + 2:SEG + 3], scalar1=cfg[:, 2:3], scalar2=None, op0=ALU.mult), [Rxin, Rcfg], [Rxin])
                    DVE(lambda e: e.tensor_scalar(out=xc[:], in0=xin[:, 0:SEG], scalar1=cols[:, c, 0:1], scalar2=cols[:, c, 4:5], op0=ALU.mult, op1=ALU.add), [Rxin, Rcols], [Rxc])
                    for j in range(1, 4):
                        DVE(lambda e, j=j: e.scalar_tensor_tensor(out=xc[:], in0=xin[:, j:j + SEG], scalar=cols[:, c, j:j + 1], in1=xc[:], op0=ALU.mult, op1=ALU.add), [Rxin, Rcols, Rxc], [Rxc])
                    ACT(lambda e: e.activation(out=xcb[:], in_=xc[:], func=AF.Copy), [Rxc], [Rxcb])
                    for gi_, (dst, Rd, bcol) in enumerate(((tA, RtA, 5 + d), (tB, RtB, 7 + d))):
                        idx = (d * 2 + gi_) * 4 + c
                        for b in range(NB):
                            pt, Rp = nps()
                            PE(lambda e, pt=pt, b=b, idx=idx: e.matmul(pt[:, :], wbd[:, idx, :], xcb[:, b * 512:(b + 1) * 512], start=True, stop=True), [Rwbd, Rxcb], [Rp])
                            ACT(lambda e, pt=pt, b=b, dst=dst, bcol=bcol: e.activation(out=dst[:, b * 512:(b + 1) * 512], in_=pt[:, :], func=AF.Sigmoid, bias=cols[:, c, bcol:bcol + 1]), [Rp, Rcols], [Rd])
                    ACT(lambda e: e.activation(out=tC[:], in_=tA[:], func=AF.Exp, scale=cols[:, c, 13 + d:14 + d]), [RtA, Rcols], [RtC])
                    ACT(lambda e: e.activation(out=tC[:], in_=tC[:], func=AF.Sqrt, scale=-1.0, bias=ONEC[:, 0:1]), [RtC, RONE], [RtC])
                    ACT(lambda e: e.activation(out=tA[:], in_=tA[:], func=AF.Exp, scale=cols[:, c, 11 + d:12 + d]), [RtA, Rcols], [RtA])
                    DVE(lambda e: e.tensor_tensor(out=tB[:], in0=tB[:], in1=tC[:], op=ALU.mult), [RtB, RtC], [RtB])
                    DVE(lambda e: e.tensor_tensor(out=tB[:], in0=tB[:], in1=xc[:], op=ALU.mult), [RtB, Rxc], [RtB])
                    if d == 0 and s == 2:
                        DVE(lambda e: e.tensor_scalar(out=tA[:, 0:1], in0=tA[:, 0:1], scalar1=cfg[:, 2:3], scalar2=None, op0=ALU.mult), [RtA, Rcfg], [RtA])
                    if d == 1 and s == 1:
                        DVE(lambda e: e.tensor_scalar(out=tA[:, SEG - 1:SEG], in0=tA[:, SEG - 1:SEG], scalar1=cfg[:, 2:3], scalar2=None, op0=ALU.mult), [RtA, Rcfg], [RtA])
                    first = (s == 0) if d == 0 else (s == 3)
                    init = 0.0 if first else car[:, c * 2 + d:c * 2 + d + 1]
                    if d == 0:
                        DVE(lambda e: e.tensor_tensor_scan(out=tC[:], data0=tA[:], data1=tB[:], initial=init, op0=ALU.mult, op1=ALU.add), [RtA, RtB, Rcar], [RtC])
                        DVE(lambda e: e.tensor_copy(out=car[:, c * 2:c * 2 + 1], in_=tC[:, SEG - 1:SEG]), [RtC], [Rcar])
                    else:
                        DVE(lambda e: e.tensor_tensor_scan(out=rev_ap(tC[:]), data0=rev_ap(tA[:]), data1=rev_ap(tB[:]), initial=init, op0=ALU.mult, op1=ALU.add), [RtA, RtB, Rcar], [RtC])
                        DVE(lambda e: e.tensor_copy(out=car[:, c * 2 + 1:c * 2 + 2], in_=tC[:, 0:1]), [RtC], [Rcar])
                for c in range(4):
                    for s in range(4):
                        seg_common(c, s, 0)
                        store(hfT[c * 128:(c + 1) * 128, s * SEG:(s + 1) * SEG], tC[:], RtC, R["hfT"])
                for c in range(4):
                    for s in (3, 2, 1, 0):
                        seg_common(c, s, 1)
                        sl = slice(s * SEG, (s + 1) * SEG)
                        load(tA[:], hfT[c * 128:(c + 1) * 128, sl], RtA, src=[R["hfT"]])
                        load(cg[:], cxT[512 + c * 128:512 + (c + 1) * 128, sl], Rcg, src=[R["cxT"]])
                        DVE(lambda e: e.tensor_tensor(out=tC[:], in0=tC[:], in1=tA[:], op=ALU.add), [RtC, RtA], [RtC])
                        ACT(lambda e: e.activation(out=tA[:], in_=cg[:], func=AF.Gelu_apprx_tanh), [Rcg, RtA], [RtA])
                        DVE(lambda e: e.tensor_tensor(out=ot[:], in0=tC[:], in1=tA[:], op=ALU.mult), [RtC, RtA], [Rot])
                        store(oT[1024 + c * 128:1024 + (c + 1) * 128, sl], ot[:], Rot, R["oT"])
            P.barrier()

        ONEC = S("onec", [128, 1]); RONE = Res()
        DVE(lambda e: e.memset(ONEC[:], 1.0), [], [RONE])

        def phaseD_gdn(l):
            BLK = 4096 if T >= 4096 else T
            with ExitStack() as ps:
                T_ = lambda n, s, d=F32: ps.enter_context(sbt(n, list(s), d))
                c8 = T_("c8", [8, 4]); Rc8 = P.dma_res("c8")
                load(c8[:, 0:1], W["d_a_log"][l].rearrange("d (h o) -> (d h) o", o=1), Rc8)
                load(c8[:, 1:2], W["d_dt_bias"][l].rearrange("d (h o) -> (d h) o", o=1), Rc8)
                ACT(lambda e: e.activation(out=c8[:, 2:3], in_=c8[:, 0:1], func=AF.Exp), [Rc8], [Rc8])
                DVE(lambda e: e.tensor_scalar(out=c8[:, 2:3], in0=c8[:, 2:3], scalar1=-1.0, scalar2=None, op0=ALU.mult), [Rc8], [Rc8])
                mF = T_("mF", [8, BLK]); mB = T_("mB", [8, BLK]); Rm = Res()
                DVE(lambda e: e.memset(mF[:], 1.0), [], [Rm]); DVE(lambda e: e.memset(mB[:], 1.0), [], [Rm])
                DVE(lambda e: e.memset(mF[:].rearrange("p (n j) -> p n j", j=64)[:, :, 0:1], 0.0), [], [Rm])
                DVE(lambda e: e.memset(mB[:].rearrange("p (n j) -> p n j", j=64)[:, :, 63:64], 0.0), [], [Rm])
                ga = T_("ga", [8, BLK]); Rga = P.dma_res("ga"); gb = T_("gb", [8, BLK]); Rgb = P.dma_res("gb")
                gp = T_("gp", [8, BLK]); Rgp = Res(); gs = T_("gs", [8, BLK]); Rgs = Res()
                for b0 in range(0, T, BLK):
                    sl = slice(b0, b0 + BLK)
                    load(ga[:], daT[:, sl], Rga, src=[R["daT"]]); load(gb[:], dbT[:, sl], Rgb, src=[R["dbT"]])
                    ACT(lambda e: e.activation(out=ga[:], in_=ga[:], func=AF.Exp, bias=c8[:, 1:2]), [Rga, Rc8], [Rga])
                    ACT(lambda e: e.activation(out=ga[:], in_=ga[:], func=AF.Ln, bias=1.0), [Rga], [Rga])
                    DVE(lambda e: e.tensor_scalar(out=ga[:], in0=ga[:], scalar1=c8[:, 2:3], scalar2=None, op0=ALU.mult), [Rga, Rc8], [Rga])
                    DVE(lambda e: e.tensor_tensor_scan(out=gp[:], data0=mF[:], data1=ga[:], initial=0.0, op0=ALU.mult, op1=ALU.add), [Rga, Rm], [Rgp])
                    DVE(lambda e: e.tensor_tensor_scan(out=rev_ap(gs[:]), data0=rev_ap(mB[:]), data1=rev_ap(ga[:]), initial=0.0, op0=ALU.mult, op1=ALU.add), [Rga, Rm], [Rgs])
                    ACT(lambda e: e.activation(out=gb[:], in_=gb[:], func=AF.Sigmoid), [Rgb], [Rgb])
                    store(gcD[0:4, sl], gp[0:4, :], Rgp, R["gcD"]); store(gcD[4:8, sl], gs[4:8, :], Rgs, R["gcD"])
                    store(gcD[8:16, sl], gb[:], Rgb, R["gcD"])
            P.barrier()
            with ExitStack() as ps:
                T_ = lambda n, s, d=F32: ps.enter_context(sbt(n, list(s), d))
                Gc = T_("Gc", [64, 16, NCH]); RGc = Res()
                gl = T_("gl", [128, 64]); Rgl = P.dma_res("gl")
                for r in range(16):
                    for n0 in range(0, NCH, 128):
                        nn = min(128, NCH - n0)
                        load(gl[0:nn, :], gcD[r, n0 * 64:(n0 + nn) * 64].rearrange("(n j) -> n j", j=64), Rgl, src=[R["gcD"]])
                        pt, Rp = nps()
                        PE(lambda e, pt=pt, nn=nn: e.transpose(out=pt[0:64, 0:nn], in_=gl[0:nn, :], identity=cst[0:nn, 0:nn]), [Rgl, Rcst], [Rp])
                        DVE(lambda e, pt=pt, nn=nn, r=r, n0=n0: e.tensor_copy(out=Gc[:, r, n0:n0 + nn], in_=pt[0:64, 0:nn]), [Rp], [RGc])
                dcol = T_("dcol", [128, 4, 12]); Rdcol = P.dma_res("dcol"); ong = T_("ong", [128, 1])
                colap = lambda a: a.rearrange("(p o) -> p o", o=1)
                for h in range(4):
                    for part in range(3):
                        for j in range(4):
                            load(dcol[:, h, part * 4 + j:part * 4 + j + 1], colap(W["d_conv_w"][l, j, part * 512 + h * 128:part * 512 + (h + 1) * 128]), Rdcol)
                load(ong[:], colap(W["d_on_g"][l]), Rdcol)
                negm = T_("negm", [64, 128]); Rnegm = Res()
                DVE(lambda e: e.tensor_scalar(out=negm[:, 0:64], in0=cst[0:64, 128:192], scalar1=-1.0, scalar2=None, op0=ALU.mult), [Rcst], [Rnegm])
                DVE(lambda e: e.tensor_scalar(out=negm[:, 64:128], in0=cst[0:64, 256:320], scalar1=-1.0, scalar2=None, op0=ALU.mult), [Rcst], [Rnegm])
                I64 = cst[0:64, 384:448]
                S32 = [T_("S32_%d" % h, [128, 128]) for h in range(4)]; Sb = [T_("Sb_%d" % h, [128, 128], BF16) for h in range(4)]
                RS32 = [Res() for _ in range(4)]; RSb = [Res() for _ in range(4)]
                qin = T_("qin", [128, 3, 516], BF16); Rqin = P.dma_res("qin")
                cv = T_("cv", [128, 3, 512]); Rcv = Res(); sqb = T_("sqb", [128, 1024], BF16); Rsqb = Res()
                rqk = T_("rqk", [128, 1024]); Rrqk = Res()
                Grow = T_("Grow", [128, 512]); RGrow = P.dma_res("Grow"); Brow = T_("Brow", [64, 512]); RBrow = P.dma_res("Brow")
                eG = T_("eG", [128, 512]); ReG = Res(); qtb = T_("qtb", [128, 512], BF16); Rqtb = Res()
                vtb = T_("vtb", [128, 512], BF16); Rvtb = Res()
                dl = T_("dl", [64, 512]); Rdl = Res(); e1 = T_("e1", [64, 512]); Re1 = Res(); e2 = T_("e2", [64, 512]); Re2 = Res()
                DL = T_("DL", [64, 512]); RDL = Res(); DLT = T_("DLT", [64, 512]); RDLT = Res(); DA = T_("DA", [64, 512]); RDA = Res()
                X = [T_("X%d" % i, [64, 512], BF16) for i in range(2)]; Y = [T_("Y%d" % i, [64, 512], BF16) for i in range(2)]
                RX = [Res(), Res()]; RY = [Res(), Res()]
                Qc = [T_("Qc%d" % i, [64, 512], BF16) for i in range(2)]; RQc = [Res(), Res()]
                sm = T_("sm", [64, 32]); Rsm = Res()
                Kt = [T_("Kt%d" % h, [128, 512], BF16) for h in range(4)]; Qg = [T_("Qg%d" % h, [128, 512], BF16) for h in range(4)]
                aT = [T_("aT%d" % h, [64, 512], BF16) for h in range(4)]; Qf = [T_("Qf%d" % h, [64, 512], BF16) for h in range(4)]
                Ktil = [T_("Ktil%d" % h, [64, 8, 128], BF16) for h in range(4)]; Vb = [T_("Vb%d" % h, [64, 8, 128]) for h in range(4)]
                cwc = [T_("cw%d" % h, [64, 8]) for h in range(4)]; El = [T_("El%d" % h, [128, 8]) for h in range(4)]
                RKt, RQg, RaT, RQf, RKtil, RVb, Rcw, REl = [[Res() for _ in range(4)] for _ in range(8)]
                Rm_ = [T_("Rm%d" % h, [64, 128], BF16) for h in range(4)]; RRm = [Res() for _ in range(4)]
                vn = [T_("vn%d" % h, [64, 128], BF16) for h in range(4)]; Rvn = [Res() for _ in range(4)]
                ofs = T_("ofs", [128, 512]); Rofs = P.dma_res("ofs"); zt = T_("zt", [128, 512], BF16); Rzt = P.dma_res("zt")
                osum = T_("osum", [128, 512]); Rosum = Res(); ostg = T_("ostgD", [128, 512], BF16); Rostg = Res()
                r2 = [0]
                def ring2():
                    i = r2[0] % 2; r2[0] += 1; return psf[i], Rpsf[i]
                v8 = lambda t: t[:, :].rearrange("p (n j) -> p n j", j=64)
                for d in range(2):
                    LAST = 63 if d == 0 else 0
                    mL = negm[:, 0:64] if d == 0 else negm[:, 64:128]
                    mLT = negm[:, 64:128] if d == 0 else negm[:, 0:64]
                    mA = cst[0:64, 320:384] if d == 0 else cst[0:64, 192:256]
                    for h in range(4):
                        DVE(lambda e, h=h: e.memset(S32[h][:], 0.0), [], [RS32[h]])
                        DVE(lambda e, h=h: e.memset(Sb[h][:], 0.0), [], [RSb[h]])
                    blocks = range(NG) if d == 0 else range(NG - 1, -1, -1)
                    for b in blocks:
                        t0 = b * 512; n0 = b * 8
                        for h in range(4):
                            r = d * 4 + h
                            DVE(lambda e: e.memset(qin[:], 0.0), [], [Rqin])
                            lo = max(0, t0 - 2); hi = min(T, t0 + 513)
                            for part in range(3):
                                load(qin[:, part, 2 - (t0 - lo):2 + (hi - t0)], dqT[part * 512 + h * 128:part * 512 + (h + 1) * 128, lo:hi], Rqin, src=[R["dqT"]])
                            if t0 == HALF:
                                DVE(lambda e: e.tensor_scalar(out=qin[:, :, 0:2], in0=qin[:, :, 0:2], scalar1=cfg[:, 2:3], scalar2=None, op0=ALU.mult), [Rqin, Rcfg], [Rqin])
                            if t0 + 512 == HALF:
                                DVE(lambda e: e.tensor_scalar(out=qin[:, :, 514:515], in0=qin[:, :, 514:515], scalar1=cfg[:, 2:3], scalar2=None, op0=ALU.mult), [Rqin, Rcfg], [Rqin])
                            for part in range(3):
                                DVE(lambda e, part=part, h=h: e.tensor_scalar(out=cv[:, part, :], in0=qin[:, part, 0:512], scalar1=dcol[:, h, part * 4:part * 4 + 1], scalar2=None, op0=ALU.mult), [Rqin, Rdcol], [Rcv])
                                for j in range(1, 4):
                                    DVE(lambda e, part=part, h=h, j=j: e.scalar_tensor_tensor(out=cv[:, part, :], in0=qin[:, part, j:j + 512], scalar=dcol[:, h, part * 4 + j:part * 4 + j + 1], in1=cv[:, part, :], op0=ALU.mult, op1=ALU.add), [Rqin, Rdcol, Rcv], [Rcv])
                            ACT(lambda e: e.activation(out=cv[:], in_=cv[:], func=AF.Silu), [Rcv], [Rcv])
                            ACT(lambda e: e.activation(out=sqb[:].rearrange("p (a t) -> p a t", t=512), in_=cv[:, 0:2, :], func=AF.Square), [Rcv], [Rsqb])
                            for a_ in range(2):
                                pt, Rp = ring2()
                                PE(lambda e, pt=pt, a_=a_: e.matmul(pt[:, :], onesb[:], sqb[:, a_ * 512:(a_ + 1) * 512], start=True, stop=True), [Rones, Rsqb], [Rp])
                                ACT(lambda e, pt=pt, a_=a_: e.activation(out=rqk[:, a_ * 512:(a_ + 1) * 512], in_=pt[:, :], func=AF.Sqrt, bias=EPSC[:, 0:1]), [Rp, REPS], [Rrqk])
                            DVE(lambda e: e.reciprocal(out=rqk[:], in_=rqk[:]), [Rrqk], [Rrqk])
                            DVE(lambda e, h=h: e.tensor_tensor(out=Kt[h][:], in0=cv[:, 1, :], in1=rqk[:, 512:1024], op=ALU.mult), [Rcv, Rrqk], [RKt[h]])
                            DVE(lambda e: e.scalar_tensor_tensor(out=cv[:, 0, :], in0=cv[:, 0, :], scalar=128.0 ** -0.5, in1=rqk[:, 0:512], op0=ALU.mult, op1=ALU.mult), [Rcv, Rrqk], [Rcv])
                            ACT(lambda e: e.activation(out=qtb[:], in_=cv[:, 0, :], func=AF.Copy), [Rcv], [Rqtb])
                            ACT(lambda e: e.activation(out=vtb[:], in_=cv[:, 2, :], func=AF.Copy), [Rcv], [Rvtb])
                            load(Grow[:], gcD[r:r + 1, t0:t0 + 512].broadcast_to([128, 512]), RGrow, src=[R["gcD"]])
                            load(Brow[:], gcD[8 + r:9 + r, t0:t0 + 512].broadcast_to([64, 512]), RBrow, src=[R["gcD"]])
                            gcol = Gc[:, r, n0:n0 + 8]; bcol_ = Gc[:, 8 + r, n0:n0 + 8]
                            ACT(lambda e: e.activation(out=eG[:], in_=Grow[:], func=AF.Exp), [RGrow], [ReG])
                            DVE(lambda e, h=h: e.tensor_tensor(out=Qg[h][:], in0=cv[:, 0, :], in1=eG[:], op=ALU.mult), [Rcv, ReG], [RQg[h]])
                            ACT(lambda e, h=h: e.activation(out=El[h][:], in_=v8(Grow)[:, :, LAST], func=AF.Exp), [RGrow], [REl[h]])
                            DVE(lambda e: e.tensor_tensor(out=v8(dl), in0=v8(Grow)[0:64], in1=bc_last(gcol, 64), op=ALU.subtract), [RGrow, RGc], [Rdl])
                            DVE(lambda e: e.tensor_scalar(out=e2[:], in0=dl[:], scalar1=0.0, scalar2=None, op0=ALU.min), [Rdl], [Re2])
                            DVE(lambda e: e.tensor_scalar(out=e1[:], in0=dl[:], scalar1=0.0, scalar2=None, op0=ALU.max), [Rdl], [Re1])
                            ACT(lambda e: e.activation(out=e2[:], in_=e2[:], func=AF.Exp), [Re2], [Re2])
                            ACT(lambda e: e.activation(out=e1[:], in_=e1[:], func=AF.Exp, scale=-1.0), [Re1], [Re1])
                            DVE(lambda e: e.tensor_tensor(out=v8(DL), in0=v8(e1), in1=bc_mid(mL, 8), op=ALU.mult), [Re1, Rnegm], [RDL])
                            DVE(lambda e: e.tensor_tensor(out=v8(DL), in0=v8(DL), in1=bc_last(bcol_, 64), op=ALU.mult), [RDL, RGc], [RDL])
                            DVE(lambda e: e.tensor_tensor(out=v8(DLT), in0=v8(e2), in1=bc_mid(mLT, 8), op=ALU.mult), [Re2, Rnegm], [RDLT])
                            DVE(lambda e: e.tensor_tensor(out=DLT[:], in0=DLT[:], in1=Brow[:], op=ALU.mult), [RDLT, RBrow], [RDLT])
                            DVE(lambda e: e.tensor_tensor(out=v8(DA), in0=v8(e2), in1=bc_mid(mA, 8), op=ALU.mult), [Re2, Rcst], [RDA])
                            ACT(lambda e: e.activation(out=sm[:, 0:8], in_=gcol, func=AF.Exp), [RGc], [Rsm])
                            DVE(lambda e, h=h: e.scalar_tensor_tensor(out=cwc[h][:], in0=sm[:, 0:8], scalar=-1.0, in1=bcol_, op0=ALU.mult, op1=ALU.mult), [Rsm, RGc], [Rcw[h]])
                            DVE(lambda e: e.tensor_tensor(out=sm[:, 8:16], in0=v8(Grow)[0:64, :, LAST], in1=gcol, op=ALU.subtract), [RGrow, RGc], [Rsm])
                            ACT(lambda e: e.activation(out=sm[:, 8:16], in_=sm[:, 8:16], func=AF.Exp), [Rsm], [Rsm])
                            pb, Rpb = npsb()
                            for n in range(8):
                                PE(lambda e, n=n, pb=pb, h=h: e.transpose(out=pb[0:64, n * 128:(n + 1) * 128], in_=Kt[h][:, n * 64:(n + 1) * 64], identity=identb[:]), [RKt[h], Ridb], [Rpb])
                            DVE(lambda e, pb=pb, h=h: e.tensor_tensor(out=Ktil[h][:], in0=pb[0:64, :].rearrange("p (n k) -> p n k", k=128), in1=bc_last(sm[:, 8:16], 128), op=ALU.mult), [Rpb, Rsm], [RKtil[h]])
                            pb, Rpb = npsb()
                            for n in range(8):
                                PE(lambda e, n=n, pb=pb: e.transpose(out=pb[0:64, n * 128:(n + 1) * 128], in_=vtb[:, n * 64:(n + 1) * 64], identity=identb[:]), [Rvtb, Ridb], [Rpb])
                            DVE(lambda e, pb=pb, h=h: e.tensor_tensor(out=Vb[h][:], in0=pb[0:64, :].rearrange("p (n k) -> p n k", k=128), in1=bc_last(bcol_, 128), op=ALU.mult), [Rpb, RGc], [RVb[h]])
                            pm1, Rpm1 = ring2()
                            for n in range(8):
                                cs = slice(n * 64, (n + 1) * 64)
                                PE(lambda e, cs=cs, pm1=pm1, h=h: e.matmul(pm1[0:64, cs], Kt[h][:, cs], Kt[h][:, cs], start=True, stop=True), [RKt[h]], [Rpm1])
                            DVE(lambda e, pm1=pm1: e.tensor_tensor(out=X[0][:], in0=pm1[0:64, :], in1=DL[:], op=ALU.mult), [Rpm1, RDL], [RX[0]])
                            DVE(lambda e, pm1=pm1: e.tensor_tensor(out=Y[0][:], in0=pm1[0:64, :], in1=DLT[:], op=ALU.mult), [Rpm1, RDLT], [RY[0]])
                            pm2, Rpm2 = ring2()
                            for n in range(8):
                                cs = slice(n * 64, (n + 1) * 64)
                                PE(lambda e, cs=cs, pm2=pm2, h=h: e.matmul(pm2[0:64, cs], Kt[h][:, cs], qtb[:, cs], start=True, stop=True), [RKt[h], Rqtb], [Rpm2])
                            DVE(lambda e, pm2=pm2, h=h: e.tensor_tensor(out=aT[h][:], in0=pm2[0:64, :], in1=DA[:], op=ALU.mult), [Rpm2, RDA], [RaT[h]])
                            DVE(lambda e: e.tensor_tensor(out=v8(Qc[0]), in0=v8(Y[0]), in1=bc_mid(I64, 8), op=ALU.add), [RY[0], Rcst], [RQc[0]])
                            cur = 0
                            for k in range(5):
                                nxt = 1 - cur
                                pX, RpX = ring2()
                                for n in range(8):
                                    cs = slice(n * 64, (n + 1) * 64)
                                    PE(lambda e, cs=cs, pX=pX, cur=cur: e.matmul(pX[0:64, cs], Y[cur][:, cs], X[cur][:, cs], start=True, stop=True), [RX[cur], RY[cur]], [RpX])
                                if k < 4:
                                    pY, RpY = ring2()
                                    for n in range(8):
                                        cs = slice(n * 64, (n + 1) * 64)
                                        PE(lambda e, cs=cs, pY=pY, cur=cur: e.matmul(pY[0:64, cs], X[cur][:, cs], Y[cur][:, cs], start=True, stop=True), [RX[cur], RY[cur]], [RpY])
                                ACT(lambda e, pX=pX, nxt=nxt: e.activation(out=X[nxt][:], in_=pX[0:64, :], func=AF.Copy), [RpX], [RX[nxt]])
                                if k < 4:
                                    ACT(lambda e, pY=pY, nxt=nxt: e.activation(out=Y[nxt][:], in_=pY[0:64, :], func=AF.Copy), [RpY], [RY[nxt]])
                                pQ, RpQ = ring2()
                                for n in range(8):
                                    cs = slice(n * 64, (n + 1) * 64)
                                    PE(lambda e, cs=cs, pQ=pQ, nxt=nxt, cur=cur: e.matmul(pQ[0:64, cs], X[nxt][:, cs], Qc[cur][:, cs], start=True, stop=True), [RX[nxt], RQc[cur]], [RpQ])
                                dstq = Qc[nxt] if k < 4 else Qf[h]
                                Rdq = RQc[nxt] if k < 4 else RQf[h]
                                DVE(lambda e, pQ=pQ, cur=cur, dstq=dstq: e.tensor_tensor(out=dstq[:], in0=pQ[0:64, :], in1=Qc[cur][:], op=ALU.add), [RpQ, RQc[cur]], [Rdq])
                                cur = nxt
                        chunks = range(8) if d == 0 else range(7, -1, -1)
                        for n in chunks:
                            cs = slice(n * 64, (n + 1) * 64)
                            tstart = t0 + n * 64
                            for h in range(4):
                                if (d == 0 and tstart == HALF) or (d == 1 and tstart + 64 == HALF):
                                    DVE(lambda e, h=h: e.tensor_scalar(out=S32[h][:], in0=S32[h][:], scalar1=cfg[:, 2:3], scalar2=None, op0=ALU.mult), [RS32[h], Rcfg], [RS32[h]])
                                    ACT(lambda e, h=h: e.activation(out=Sb[h][:], in_=S32[h][:], func=AF.Copy), [RS32[h]], [RSb[h]])
                                po, Rpo = psf[2 + h], Rpsf[2 + h]
                                p1, Rp1 = ring2()
                                PE(lambda e, p1=p1, h=h, cs=cs: e.matmul(p1[0:64, 0:128], Kt[h][:, cs], Sb[h][:], start=True, stop=True), [RKt[h], RSb[h]], [Rp1])
                                DVE(lambda e, p1=p1, h=h, n=n: e.scalar_tensor_tensor(out=Rm_[h][:], in0=p1[0:64, 0:128], scalar=cwc[h][:, n:n + 1], in1=Vb[h][:, n, :], op0=ALU.mult, op1=ALU.add), [Rp1, Rcw[h], RVb[h]], [RRm[h]])
                                p2, Rp2 = ring2()
                                PE(lambda e, p2=p2, h=h, cs=cs: e.matmul(p2[0:64, 0:128], Qf[h][:, cs], Rm_[h][:], start=True, stop=True), [RQf[h], RRm[h]], [Rp2])
                                ACT(lambda e, p2=p2, h=h: e.activation(out=vn[h][:], in_=p2[0:64, 0:128], func=AF.Copy), [Rp2], [Rvn[h]])
                                PE(lambda e, po=po, h=h, cs=cs: e.matmul(po[:, cs], Sb[h][:], Qg[h][:, cs], start=True, stop=False), [RSb[h], RQg[h]], [Rpo])
                                PE(lambda e, po=po, h=h, cs=cs: e.matmul(po[:, cs], vn[h][:], aT[h][:, cs], start=False, stop=True), [Rvn[h], RaT[h]], [Rpo])
                                p3, Rp3 = ring2()
                                PE(lambda e, p3=p3, h=h, n=n: e.matmul(p3[:, 0:128], Ktil[h][:, n, :], vn[h][:], start=True, stop=True), [RKtil[h], Rvn[h]], [Rp3])
                                DVE(lambda e, p3=p3, h=h, n=n: e.scalar_tensor_tensor(out=S32[h][:], in0=S32[h][:], scalar=El[h][:, n:n + 1], in1=p3[:, 0:128], op0=ALU.mult, op1=ALU.add), [Rp3, REl[h], RS32[h]], [RS32[h]])
                                ACT(lambda e, h=h: e.activation(out=Sb[h][:], in_=S32[h][:], func=AF.Copy), [RS32[h]], [RSb[h]])
                        for h in range(4):
                            po, Rpo = psf[2 + h], Rpsf[2 + h]
                            rows = slice(h * 128, (h + 1) * 128)
                            if d == 0:
                                DVE(lambda e, po=po: e.tensor_copy(out=osum[:], in_=po[:, :]), [Rpo], [Rosum])
                                store(ofT[rows, t0:t0 + 512], osum[:], Rosum, R["ofT"])
                            else:
                                load(ofs[:], ofT[rows, t0:t0 + 512], Rofs, src=[R["ofT"]])
                                load(zt[:], dqT[1536 + h * 128:1536 + (h + 1) * 128, t0:t0 + 512], Rzt, src=[R["dqT"]])
                                DVE(lambda e, po=po: e.tensor_tensor(out=osum[:], in0=po[:, :], in1=ofs[:], op=ALU.add), [Rpo, Rofs], [Rosum])
                                ACT(lambda e: e.activation(out=sqb[:, 0:512], in_=osum[:], func=AF.Square), [Rosum], [Rsqb])
                                pt, Rp = ring2()
                                PE(lambda e, pt=pt: e.matmul(pt[:, :], onesb[:], sqb[:, 0:512], start=True, stop=True), [Rones, Rsqb], [Rp])
                                ACT(lambda e, pt=pt: e.activation(out=rqk[:, 0:512], in_=pt[:, :], func=AF.Sqrt, scale=1.0 / 128, bias=EPSC[:, 0:1]), [Rp, REPS], [Rrqk])
                                DVE(lambda e: e.reciprocal(out=rqk[:, 0:512], in_=rqk[:, 0:512]), [Rrqk], [Rrqk])
                                DVE(lambda e: e.scalar_tensor_tensor(out=osum[:], in0=osum[:], scalar=ong[:, 0:1], in1=rqk[:, 0:512], op0=ALU.mult, op1=ALU.mult), [Rosum, Rrqk, Rdcol], [Rosum])
                                ACT(lambda e: e.activation(out=rqk[:, 512:1024], in_=zt[:], func=AF.Silu), [Rzt, Rrqk], [Rrqk])
                                DVE(lambda e: e.tensor_tensor(out=ostg[:], in0=osum[:], in1=rqk[:, 512:1024], op=ALU.mult), [Rosum, Rrqk], [Rostg])
                                store(oT[1536 + h * 128:1536 + (h + 1) * 128, t0:t0 + 512], ostg[:], Rostg, R["oT"])
            P.barrier()

        def phaseM(l, xsrc, xres, xdst, xdres):
            with ExitStack() as ps:
                T_ = lambda n, s, d=F32: ps.enter_context(sbt(n, list(s), d))
                TG = 256
                wg = T_("wg", [128, 8, 4096], BF16); RwA = P.dma_res("wM")
                for kc in range(8):
                    load(wg[:, kc, :], wb["w_in"][l, kc * 128:(kc + 1) * 128, 5040:9136], RwA, src=[R_wb])
                wbr = T_("wbr", [128, 16, 1024], BF16); wo = T_("wo", [128, 8, 1024], BF16)
                load(wbr[:], wb["w_branch"][l].rearrange("(k p) n -> p k n", p=128), RwA, src=[R_wb])
                load(wo[:], wb["w_out"][l].rearrange("(k p) n -> p k n", p=128), RwA, src=[R_wb])
                gt = T_("gt", [128, D]); Rg = P.dma_res("gtM")
                load(gt[:], W["ln1_g"][l:l + 1, :].broadcast_to([128, D]), Rg)
                xt = T_("xt", [128, 2, D]); Rxt = P.dma_res("xtM")
                xnb = T_("xnb", [128, 2, D], BF16); Rxnb = Res()
                xnT = T_("xnT", [128, 8, TG], BF16); RxnT = Res()
                junk = T_("junk", [128, D]); Rjunk = Res(); st = T_("st", [128, 24]); Rst = Res()
                oTt = T_("oTt", [128, 16, TG], BF16); RoTt = P.dma_res("oTt")
                sg = T_("sg", [128, TG]); Rsg = Res(); acc = T_("acc", [128, TG]); Racc = Res(); tmp = T_("tmpM", [128, TG]); Rtmp = Res()
                mT = T_("mT", [128, 8, TG], BF16); RmT = Res()
                xo = [T_("xo%d" % i, [128, 2, D]) for i in range(2)]; Rxo = [Res(), Res()]
                for g in range(T // TG):
                    t0 = g * TG; ob_ = g % 2
                    norm_T("M", xsrc, t0, TG, gt, Rg, xt, Rxt, xnb, Rxnb, xnT, RxnT, junk, Rjunk, st, Rst, [xres])
                    load(oTt[:], oT[:, t0:t0 + TG].rearrange("(k p) t -> p k t", p=128), RoTt, src=[R["oT"]])
                    for m in range(8):
                        for br in range(4):
                            pg, Rpg = nps()
                            for kc in range(8):
                                PE(lambda e, pg=pg, kc=kc, br=br, m=m: e.matmul(pg[:, 0:TG], wg[:, kc, br * 1024 + m * 128:br * 1024 + (m + 1) * 128], xnT[:, kc, :], start=(kc == 0), stop=(kc == 7)), [RwA, RxnT], [Rpg])
                            ACT(lambda e, pg=pg: e.activation(out=sg[:], in_=pg[:, 0:TG], func=AF.Sigmoid), [Rpg], [Rsg])
                            pb_, Rpb_ = nps()
                            for kc in range(4):
                                PE(lambda e, pb_=pb_, kc=kc, br=br, m=m: e.matmul(pb_[:, 0:TG], wbr[:, br * 4 + kc, m * 128:(m + 1) * 128], oTt[:, br * 4 + kc, :], start=(kc == 0), stop=(kc == 3)), [RwA, RoTt], [Rpb_])
                            if br == 0:
                                DVE(lambda e, pb_=pb_: e.tensor_tensor(out=acc[:], in0=pb_[:, 0:TG], in1=sg[:], op=ALU.mult), [Rpb_, Rsg], [Racc])
                            else:
                                DVE(lambda e, pb_=pb_: e.tensor_tensor(out=tmp[:], in0=pb_[:, 0:TG], in1=sg[:], op=ALU.mult), [Rpb_, Rsg], [Rtmp])
                                if br < 3:
                                    DVE(lambda e: e.tensor_tensor(out=acc[:], in0=acc[:], in1=tmp[:], op=ALU.add), [Racc, Rtmp], [Racc])
                                else:
                                    DVE(lambda e, m=m: e.tensor_tensor(out=mT[:, m, :], in0=acc[:], in1=tmp[:], op=ALU.add), [Racc, Rtmp], [RmT])
                    for j in range(TG // 128):
                        for nh in range(2):
                            pt, Rp = nps()
                            for kc in range(8):
                                PE(lambda e, pt=pt, kc=kc, j=j, nh=nh: e.matmul(pt[:, :], mT[:, kc, j * 128:(j + 1) * 128], wo[:, kc, nh * 512:(nh + 1) * 512], start=(kc == 0), stop=(kc == 7)), [RmT, RwA], [Rp])
                            DVE(lambda e, pt=pt, j=j, nh=nh, ob_=ob_: e.tensor_tensor(out=xo[ob_][:, j, nh * 512:(nh + 1) * 512], in0=pt[:, :], in1=xt[:, j, nh * 512:(nh + 1) * 512], op=ALU.add), [Rp, Rxt], [Rxo[ob_]])
                    store(xdst[t0:t0 + TG, :].rearrange("(j p) d -> p j d", p=128), xo[ob_][:], Rxo[ob_], xdres)
            P.barrier()

        def phaseF(l, xsrc, xres, xdst, xdres):
            with ExitStack() as ps:
                T_ = lambda n, s, d=F32: ps.enter_context(sbt(n, list(s), d))
                TG = 256
                wu = T_("wu", [128, 8, 2 * FF], BF16); RwA = P.dma_res("wF")
                for kc in range(8):
                    load(wu[:, kc, :], wb["w_up"][l, kc * 128:(kc + 1) * 128, :], RwA, src=[R_wb])
                wd = T_("wd", [128, 22, 1024], BF16)
                load(wd[:], wb["w_down"][l].rearrange("(k p) n -> p k n", p=128), RwA, src=[R_wb])
                gt = T_("gt", [128, D]); Rg = P.dma_res("gtF")
                load(gt[:], W["ln2_g"][l:l + 1, :].broadcast_to([128, D]), Rg)
                xt = T_("xt", [128, 2, D]); Rxt = P.dma_res("xtF")
                xnb = T_("xnb", [128, 2, D], BF16); Rxnb = Res()
                xnT = T_("xnT", [128, 8, TG], BF16); RxnT = Res()
                junk = T_("junk", [128, D]); Rjunk = Res(); st = T_("st", [128, 24]); Rst = Res()
                sg = [T_("sgF%d" % i, [128, TG]) for i in range(2)]; Rsg = [Res(), Res()]
                hT = T_("hT", [128, 22, TG], BF16); RhT = Res()
                xo = [T_("xo%d" % i, [128, 2, D]) for i in range(2)]; Rxo = [Res(), Res()]
                for g in range(T // TG):
                    t0 = g * TG; ob_ = g % 2
                    norm_T("F", xsrc, t0, TG, gt, Rg, xt, Rxt, xnb, Rxnb, xnT, RxnT, junk, Rjunk, st, Rst, [xres])
                    for c in range(22):
                        pg, Rpg = nps()
                        for kc in range(8):
                            PE(lambda e, pg=pg, kc=kc, c=c: e.matmul(pg[:, 0:TG], wu[:, kc, c * 128:(c + 1) * 128], xnT[:, kc, :], start=(kc == 0), stop=(kc == 7)), [RwA, RxnT], [Rpg])
                        pu, Rpu = nps()
                        for kc in range(8):
                            PE(lambda e, pu=pu, kc=kc, c=c: e.matmul(pu[:, 0:TG], wu[:, kc, FF + c * 128:FF + (c + 1) * 128], xnT[:, kc, :], start=(kc == 0), stop=(kc == 7)), [RwA, RxnT], [Rpu])
                        sb_ = c % 2
                        ACT(lambda e, pg=pg, sb_=sb_: e.activation(out=sg[sb_][:], in_=pg[:, 0:TG], func=AF.Silu), [Rpg], [Rsg[sb_]])
                        DVE(lambda e, pu=pu, c=c, sb_=sb_: e.tensor_tensor(out=hT[:, c, :], in0=pu[:, 0:TG], in1=sg[sb_][:], op=ALU.mult), [Rpu, Rsg[sb_]], [RhT])
                    for j in range(TG // 128):
                        for nh in range(2):
                            pt, Rp = nps()
                            for kc in range(22):
                                PE(lambda e, pt=pt, kc=kc, j=j, nh=nh: e.matmul(pt[:, :], hT[:, kc, j * 128:(j + 1) * 128], wd[:, kc, nh * 512:(nh + 1) * 512], start=(kc == 0), stop=(kc == 21)), [RhT, RwA], [Rp])
                            DVE(lambda e, pt=pt, j=j, nh=nh, ob_=ob_: e.tensor_tensor(out=xo[ob_][:, j, nh * 512:(nh + 1) * 512], in0=pt[:, :], in1=xt[:, j, nh * 512:(nh + 1) * 512], op=ALU.add), [Rp, Rxt], [Rxo[ob_]])
                    store(xdst[t0:t0 + TG, :].rearrange("(j p) d -> p j d", p=128), xo[ob_][:], Rxo[ob_], xdres)
            P.barrier()

        Rxin = P.dma_res("x_in_dummy")
        phase0()
        cur, curR = x_in, Rxin
        for l in range(NL):
            run = (lambda nm: phases is None or nm in phases)
            if run("A1"): phaseA1(l, cur, curR)
            if run("A2"): phaseA2(l, cur, curR)
            if run("mla"):
                attention("b", lambda h: bqT[h], lambda h: bkT[h], lambda h: bv[:, h * 64:(h + 1) * 64], R["bqT"], R["bkT"], R["bv"], 96, 96.0 ** -0.5, 512, False)
            if run("dil"):
                attention("a", lambda h: aqT[h // 2, (h % 2) * 64:(h % 2) * 64 + 64, :], lambda h: akT[h // 2, (h % 2) * 64:(h % 2) * 64 + 64, :],
                          lambda h: av[:, h * 64:(h + 1) * 64], R["aqT"], R["akT"], R["av"], 64, 0.125, 0, True)
            if run("lru"): phaseC_rglru(l)
            if run("gdn"): phaseD_gdn(l)
            if run("M"): phaseM(l, cur, curR, x1, R["x1"])
            last = (l == NL - 1)
            dst, dR = (y_out, R["y"]) if last else (x2, R["x2"])
            if run("F"): phaseF(l, x1, R["x1"], dst, dR)
            cur, curR = x2, R["x2"]
        P.barrier()
    return nc, P


def _consts(T):
    HALF = T // 2
    cst = np.zeros((128, 512), np.float32)
    cst[:, 0:128] = np.eye(128, dtype=np.float32)
    p = np.arange(64)[:, None]; f = np.arange(64)[None, :]
    cst[0:64, 128:192] = (p > f); cst[0:64, 192:256] = (p >= f)
    cst[0:64, 256:320] = (p < f); cst[0:64, 320:384] = (p <= f)
    cst[0:64, 384:448] = np.eye(64, dtype=np.float32)
    cst[64:128, 448:512] = np.eye(64, dtype=np.float32)
    pp = np.arange(128)[:, None]; ff = np.arange(512)[None, :]
    am = np.zeros((20, 128, 512), np.float32)
    for m in range(20):
        rel = 128 * m - 1024 + pp - ff
        a = np.abs(rel)
        am[m] = (a <= 64).astype(np.float32) + ((rel % 4 == 0) & (a <= 256)) + ((rel % 16 == 0) & (a <= 1024))
    return cst, am


def _rope(pos, dim):
    inv = (1.0 / (np.float32(500000.0) ** (np.arange(0, dim, 2, dtype=np.float32) / np.float32(dim)))).astype(np.float32)
    ang = pos.astype(np.float32)[:, None] * inv[None, :]
    return np.concatenate([np.cos(ang), np.sin(ang)], axis=1).astype(np.float32)


_CACHE = {}


def run_units(units, links, T, weights, debug=False, phases=None, NL=2, ncores=8):
    key = (T, tuple(debug) if debug else None, tuple(phases) if phases else None, NL)
    if key not in _CACHE:
        _CACHE[key] = build(T, NL=NL, debug=debug, phases=phases)
    nc, P = _CACHE[key]
    cst, am = _consts(T)
    HALF = T // 2
    in_maps = []
    for c in range(ncores):
        u = c if c < len(units) else 0
        link = float(links[u])
        cfg = np.zeros((128, 4), np.float32)
        cfg[:, 1] = 0.0 if link else NEG
        cfg[:, 2] = link
        pos = np.arange(T) if link else np.concatenate([np.arange(HALF), np.arange(HALF)])
        m = {"x": np.ascontiguousarray(units[u], dtype=np.float32), "cfg": cfg, "ropeA": _rope(pos, 16), "ropeB": _rope(pos, 32),
             "amask": am, "cst": cst}
        for n, s in WSPECS:
            m[n] = np.ascontiguousarray(weights[n], dtype=np.float32).reshape(s)
        in_maps.append(m)
    res = run_bass_kernel_spmd(nc, in_maps, core_ids=list(range(ncores)))
    return res.results


def kernel(**inputs):
    xp = np.asarray(inputs["x_prompt"], dtype=np.float32)
    xs = np.asarray(inputs["x_sample"], dtype=np.float32)
    T = xs.shape[1]
    units = [xp[2 * i:2 * i + 2].reshape(T, D) for i in range(xp.shape[0] // 2)] + [xs[i] for i in range(xs.shape[0])]
    links = [0.0] * (xp.shape[0] // 2) + [1.0] * xs.shape[0]
    weights = {n: inputs[n] for n, _ in WSPECS}
    r = run_units(units, links, T, weights)
    npair = xp.shape[0] // 2
    yp = np.stack([r[i]["y"] for i in range(npair)], 0).reshape(xp.shape)
    ys = np.stack([r[npair + i]["y"] for i in range(xs.shape[0])], 0)
    return (yp.astype(np.float32), ys.astype(np.float32))
```

```python
import numpy as np
import concourse.bass as bass
import concourse.mybir as mybir
from concourse.ap import AP
from concourse.bass_utils import run_bass_kernel_spmd
from contextlib import ExitStack

F32 = mybir.dt.float32; BF16 = mybir.dt.bfloat16
AF = mybir.ActivationFunctionType; ALU = mybir.AluOpType; AX = mybir.AxisListType
import os
_SK = set(os.environ.get('K_SKIP', '').split(','))
D = 1024; IN_DIM = 9136; FF = 2816; EPS = 1e-6
NEG = -30000.0

class Res:
    __slots__ = ("lw", "rd", "sem", "cnt", "name", "ex")
    def __init__(self, name="", ex=False):
        self.lw = None; self.rd = []; self.sem = None; self.cnt = 0; self.name = name; self.ex = ex

COMPUTE = ("pe", "act", "dve", "pool")
class Prog:
    def __init__(self, nc):
        self.nc = nc
        self.ops = {k: [] for k in ("pe", "act", "dve", "pool", "sp")}
        self.cnt = {k: 0 for k in COMPUTE}
        self.esem = {}
        self.waited = {k: {} for k in self.ops}
        self.dres = []
        self.nops = 0
        for k in COMPUTE:
            self.esem[k] = nc.alloc_semaphore(name="e_" + k)
    def eng(self, q):
        nc = self.nc
        return {"pe": nc.tensor, "act": nc.scalar, "dve": nc.vector, "pool": nc.gpsimd, "sp": nc.sync}[q]
    def dma_res(self, name):
        if not hasattr(self, "named"): self.named = {}
        if name in self.named: return self.named[name]
        r = Res(name); self.named[name] = r; r.sem = self.nc.alloc_semaphore(name="d_%s_%d" % (name, len(self.dres))); self.dres.append(r); return r
    def op(self, eng, fn, reads=(), writes=(), dma=None, q="sp"):
        deps = []
        for r in reads:
            if r.lw is not None: deps.append(r.lw)
            if r.ex: deps.extend(r.rd)
        for w in writes:
            if w.lw is not None: deps.append(w.lw)
            deps.extend(w.rd)
        if eng == "dma":
            queue = q
            dma.cnt += 16
            ev = (dma.sem, dma.cnt, "dma")
            inc = (dma.sem, 16)
        else:
            queue = eng
            self.cnt[eng] += 1
            ev = (self.esem[eng], self.cnt[eng], eng)
            inc = (self.esem[eng], 1)
        waits = []
        wd = self.waited[queue]
        for (sem, val, src) in deps:
            if src == queue and eng == "pe":
                continue
            key = id(sem)
            if wd.get(key, 0) >= val:
                continue
            wd[key] = val
            waits.append((sem, val))
        e = self.eng(queue)
        for sem, val in waits:
            e.wait_ge(sem, val)
        fn(e).then_inc(inc[0], inc[1])
        self.ops[queue].append(None)
        self.nops += 1
        for r in reads: r.rd.append(ev)
        for w in writes:
            w.lw = ev; w.rd = []
        return ev
    def barrier(self):
        evs = [(self.esem[k], self.cnt[k]) for k in COMPUTE if self.cnt[k] > 0]
        for r in self.dres:
            if r.cnt > 0: evs.append((r.sem, r.cnt))
        for qn in self.ops:
            wd = self.waited[qn]
            waits = []
            for sem, val in evs:
                if qn in COMPUTE and sem is self.esem[qn]: continue
                if wd.get(id(sem), 0) >= val: continue
                wd[id(sem)] = val; waits.append((sem, val))
            e = self.eng(qn)
            for sem, val in waits:
                e.wait_ge(sem, val)
    def emit(self, block):
        def run(e, lst):
            for fn, waits, inc in lst:
                for sem, val in waits:
                    e.wait_ge(sem, val)
                if fn is not None:
                    fn(e).then_inc(inc[0], inc[1])
        @block.tensor
        def _(e): run(e, self.ops["pe"])
        @block.scalar
        def _(e): run(e, self.ops["act"])
        @block.vector
        def _(e): run(e, self.ops["dve"])
        @block.gpsimd
        def _(e): run(e, self.ops["pool"])
        @block.sync
        def _(e): run(e, self.ops["sp"])

def rev_ap(ap2d):
    n = ap2d.shape[-1]
    a = ap2d.ap
    return AP(ap2d.tensor, ap2d.offset + (n - 1) * a[-1][0], [list(a[0]), [-a[-1][0], n]])

WSPECS = [("ln1_g", [2, 1024]), ("w_in", [2, 1024, 9136]), ("a_qn_g", [2, 64]), ("a_kn_g", [2, 64]),
          ("b_qa_g", [2, 256]), ("b_wuq", [2, 256, 768]), ("b_kva_g", [2, 128]), ("b_wukv", [2, 128, 1024]),
          ("b_qn_g", [2, 96]), ("b_kn_g", [2, 96]), ("c_conv_w", [2, 4, 512]), ("c_conv_b", [2, 512]),
          ("c_wr", [2, 2, 8, 64, 64]), ("c_br", [2, 2, 512]), ("c_wi", [2, 2, 8, 64, 64]), ("c_bi", [2, 2, 512]),
          ("c_lam", [2, 2, 512]), ("d_conv_w", [2, 4, 1536]), ("d_a_log", [2, 2, 4]), ("d_dt_bias", [2, 2, 4]),
          ("d_on_g", [2, 128]), ("w_branch", [2, 2048, 1024]), ("w_out", [2, 1024, 1024]), ("ln2_g", [2, 1024]),
          ("w_up", [2, 1024, 5632]), ("w_down", [2, 2816, 1024])]

def bc_mid(ap2d, n):
    a = ap2d.ap
    return AP(ap2d.tensor, ap2d.offset, [list(a[0]), [0, n], list(a[-1])])

def bc_last(ap2d, n):
    a = ap2d.ap
    return AP(ap2d.tensor, ap2d.offset, [list(a[0]), list(a[-1]), [0, n]])

def build(T, NL=2, debug=False, phases=None):
    HALF = T // 2; NG = T // 512; NT = T // 128; NCH = T // 64
    nc = bass.Bass("TRN2", target_bir_lowering=False)
    P = Prog(nc)
    es = ExitStack()
    uid = [0]
    def sbt(name, shape, dt=F32):
        uid[0] += 1
        return nc.sbuf_tensor('%s_%d' % (name, uid[0]), shape, dt)
    def din(name, shape, dt=F32): return nc.dram_tensor(name, list(shape), dt, kind="ExternalInput").ap()
    def dsc(name, shape, dt): return nc.dram_tensor(name, list(shape), dt, kind=("ExternalOutput" if (debug and name in debug) else "Internal")).ap()
    x_in = din("x", [T, D]); cfg_in = din("cfg", [128, 4]); ropeA = din("ropeA", [T, 16]); ropeB = din("ropeB", [T, 32])
    amask_in = din("amask", [20, 128, 512]); cst_in = din("cst", [128, 512])
    W = {n: din(n, s) for n, s in WSPECS}
    y_out = nc.dram_tensor("y", [T, D], F32, kind="ExternalOutput").ap()
    wb = {"w_in": dsc("w_in_bf", [2, 1024, IN_DIM], BF16), "b_wuq": dsc("wuq_bf", [2, 256, 768], BF16),
          "b_wukv": dsc("wukv_bf", [2, 128, 1024], BF16), "w_branch": dsc("wbr_bf", [2, 2048, 1024], BF16),
          "w_out": dsc("wout_bf", [2, 1024, 1024], BF16), "w_up": dsc("wup_bf", [2, 1024, 5632], BF16),
          "w_down": dsc("wdn_bf", [2, 2816, 1024], BF16)}
    R_wb = P.dma_res("wb")
    aqT = dsc("aqT", [4, 128, T], BF16); akT = dsc("akT", [4, 128, T], BF16); av = dsc("av", [T, 512], BF16)
    bqT = dsc("bqT", [8, 96, T], BF16); bkT = dsc("bkT", [8, 96, T], BF16); bv = dsc("bv", [T, 512], BF16)
    cxT = dsc("cxT", [1024, T], BF16)
    dqT = dsc("dqT", [2048, T], BF16)
    daT = dsc("daT", [8, T], F32); dbT = dsc("dbT", [8, T], F32)
    oT = dsc("oT", [2048, T], BF16)
    x1 = dsc("x1", [T, D], F32); x2 = dsc("x2", [T, D], F32)
    hfT = dsc("hfT", [512, T], F32)
    gcD = dsc("gcD", [16, T], F32)
    ofT = dsc("ofT", [512, T], F32)
    R = {n: P.dma_res(n) for n in ["aqT", "akT", "av", "bqT", "bkT", "bv", "cxT", "dqT", "daT", "dbT", "oT", "x1", "x2", "hfT", "gcD", "ofT", "y"]}

    def S(name, shape, dt=F32):
        return es.enter_context(sbt(name, list(shape), dt))
    def dma(out, in_, reads, writes, res, q="sp"):
        P.op("dma", lambda e: e.dma_start(out=out, in_=in_), reads=reads, writes=writes, dma=res, q=q)
    def load(tile_ap, dram_ap, res, src=()):
        dma(tile_ap, dram_ap, list(src), [res], res)
    def store(dram_ap, tile_ap, tres, dres):
        dma(dram_ap, tile_ap, [tres], [dres], dres, q="pool")
    def DVE(f, r=(), w=()): P.op("dve", f, r, w)
    def ACT(f, r=(), w=()): P.op("act", f, r, w)
    def PE(f, r=(), w=()): P.op("pe", f, r, w)
    def POOL(f, r=(), w=()): P.op("pool", f, r, w)

    with es:
        psf = [es.enter_context(nc.psum_tensor("psf%d" % i, [128, 512], F32)) for i in range(6)]
        psb = [es.enter_context(nc.psum_tensor("psb%d" % i, [128, 1024], BF16)) for i in range(2)]
        Rpsf = [Res(ex=True) for _ in range(6)]; Rpsb = [Res(ex=True) for _ in range(2)]
        pctr = [0, 0]
        def nps():
            i = pctr[0] % 6; pctr[0] += 1; return psf[i], Rpsf[i]
        def npsb():
            i = pctr[1] % 2; pctr[1] += 1; return psb[i], Rpsb[i]
        cst = S("cst", [128, 512]); Rcst = P.dma_res("cst")
        load(cst[:], cst_in[:, :], Rcst)
        cfg = S("cfg", [128, 4]); Rcfg = P.dma_res("cfg")
        load(cfg[:], cfg_in[:, :], Rcfg)
        identb = S("identb", [128, 128], BF16); Ridb = Res()
        DVE(lambda e: e.tensor_copy(out=identb[:], in_=cst[:, 0:128]), [Rcst], [Ridb])
        onesb = S("onesb", [128, 128], BF16); Rones = Res()
        DVE(lambda e: e.memset(onesb[:], 1.0), [], [Rones])
        ident = cst[:, 0:128]
        CONSTS = [Rcst, Rcfg, Ridb, Rones]

        def phase0():
            with ExitStack() as ps:
                tin = [ps.enter_context(sbt("cv_in%d" % i, [128, 2048], F32)) for i in range(2)]
                tout = [ps.enter_context(sbt("cv_out%d" % i, [128, 2048], BF16)) for i in range(2)]
                Rin = [P.dma_res("cvi%d" % i) for i in range(2)]; Rout = [Res(), Res()]
                k = 0
                for name, dst in wb.items():
                    src = W[name]
                    rows, cols = src.shape[1], src.shape[2]
                    for l in range(NL):
                        for r0 in range(0, rows, 128):
                            for c0 in range(0, cols, 2048):
                                cw = min(2048, cols - c0); b = k % 2; k += 1
                                load(tin[b][:, 0:cw], src[l, r0:r0 + 128, c0:c0 + cw], Rin[b])
                                if b == 0:
                                    DVE(lambda e, b=b, cw=cw: e.tensor_copy(out=tout[b][:, 0:cw], in_=tin[b][:, 0:cw]), [Rin[b]], [Rout[b]])
                                else:
                                    ACT(lambda e, b=b, cw=cw: e.activation(out=tout[b][:, 0:cw], in_=tin[b][:, 0:cw], func=AF.Copy), [Rin[b]], [Rout[b]])
                                store(dst[l, r0:r0 + 128, c0:c0 + cw], tout[b][:, 0:cw], Rout[b], R_wb)
            P.barrier()

        def norm_T(ph, src_ap, t0, ntok, gt, Rg, xt, Rxt, xnb, Rxnb, xnT, RxnT, junk, Rjunk, st, Rst, srcres):
            nj = ntok // 128
            load(xt[:, 0:nj, :], src_ap[t0:t0 + ntok, :].rearrange("(j p) d -> p j d", p=128), Rxt, src=srcres)
            for j in range(nj):
                ACT(lambda e, j=j: e.activation(out=junk[:], in_=xt[:, j, :], func=AF.Square), [Rxt], [Rjunk])
                DVE(lambda e, j=j: e.reduce_sum(out=st[:, j:j + 1], in_=junk[:], axis=AX.X), [Rjunk], [Rst])
            ACT(lambda e: e.activation(out=st[:, 8:8 + nj], in_=st[:, 0:nj], func=AF.Sqrt, scale=1.0 / D, bias=EPSC[:, 0:1]), [Rst, REPS], [Rst])
            DVE(lambda e: e.reciprocal(out=st[:, 16:16 + nj], in_=st[:, 8:8 + nj]), [Rst], [Rst])
            for j in range(nj):
                DVE(lambda e, j=j: e.scalar_tensor_tensor(out=xnb[:, j, :], in0=xt[:, j, :], scalar=st[:, 16 + j:17 + j], in1=gt[:],
                                                          op0=ALU.mult, op1=ALU.mult), [Rxt, Rst, Rg], [Rxnb])
            for kc in range(8):
                pb, Rpb = npsb()
                for j in range(nj):
                    PE(lambda e, j=j, kc=kc, pb=pb: e.transpose(out=pb[:, j * 128:(j + 1) * 128], in_=xnb[:, j, kc * 128:(kc + 1) * 128], identity=identb[:]),
                       [Rxnb, Ridb], [Rpb])
                if kc % 2 == 0:
                    DVE(lambda e, kc=kc, pb=pb: e.tensor_copy(out=xnT[:, kc, 0:ntok], in_=pb[:, 0:ntok]), [Rpb], [RxnT])
                else:
                    ACT(lambda e, kc=kc, pb=pb: e.activation(out=xnT[:, kc, 0:ntok], in_=pb[:, 0:ntok], func=AF.Copy), [Rpb], [RxnT])

        EPSC = S("epsc", [128, 1]); REPS = Res()
        DVE(lambda e: e.memset(EPSC[:], EPS), [], [REPS])

        def bcast_rows(dst_tile, src_row_ap, reps, width, res):
            src = src_row_ap.rearrange("(o w) -> o w", o=1).broadcast_to([128, width])
            for r_ in range(reps):
                load(dst_tile[:, r_ * width:(r_ + 1) * width], src, res)

        def phaseA1(l, xsrc, xres):
            with ExitStack() as ps:
                T_ = lambda n, s, d=F32: ps.enter_context(sbt(n, list(s), d))
                NC1 = 3088
                wA = T_("wA1", [128, 8, NC1], BF16); RwA = P.dma_res("wA1")
                for kc in range(8):
                    load(wA[:, kc, :], wb["w_in"][l, kc * 128:(kc + 1) * 128, 1952:5040], RwA, src=[R_wb])
                gt = T_("gt", [128, D]); Rg = P.dma_res("gtA1")
                load(gt[:], W["ln1_g"][l:l + 1, :].broadcast_to([128, D]), Rg)
                xt = T_("xt", [128, 4, D]); Rxt = P.dma_res("xtA1")
                xnb = T_("xnb", [128, 4, D], BF16); Rxnb = Res()
                xnT = T_("xnT", [128, 8, 512], BF16); RxnT = Res()
                junk = T_("junk", [128, D]); Rjunk = Res(); st = T_("st", [128, 24]); Rst = Res()
                stg = [T_("stgA1_%d" % i, [128, 24, 512], BF16) for i in range(2)]; Rstg = [Res(), Res()]
                sab = [T_("sab%d" % i, [8, 2, 512]) for i in range(2)]; Rsab = [Res(), Res()]
                for g in range(NG):
                    t0 = g * 512; b = g % 2
                    norm_T("A1", xsrc, t0, 512, gt, Rg, xt, Rxt, xnb, Rxnb, xnT, RxnT, junk, Rjunk, st, Rst, [xres])
                    for c in range(24):
                        pt, Rp = nps()
                        for kc in range(8):
                            PE(lambda e, c=c, kc=kc, pt=pt: e.matmul(pt[:, :], wA[:, kc, c * 128:(c + 1) * 128], xnT[:, kc, :], start=(kc == 0), stop=(kc == 7)),
                               [RwA, RxnT], [Rp])
                        if c % 2 == 0:
                            DVE(lambda e, c=c, pt=pt, b=b: e.tensor_copy(out=stg[b][:, c, :], in_=pt[:, :]), [Rp], [Rstg[b]])
                        else:
                            ACT(lambda e, c=c, pt=pt, b=b: e.activation(out=stg[b][:, c, :], in_=pt[:, :], func=AF.Copy), [Rp], [Rstg[b]])
                    for i2 in range(2):
                        pt, Rp = nps()
                        for kc in range(8):
                            PE(lambda e, kc=kc, pt=pt, i2=i2: e.matmul(pt[0:8, :], wA[:, kc, 3072 + 8 * i2:3080 + 8 * i2], xnT[:, kc, :], start=(kc == 0), stop=(kc == 7)),
                               [RwA, RxnT], [Rp])
                        DVE(lambda e, pt=pt, i2=i2, b=b: e.tensor_copy(out=sab[b][:, i2, :], in_=pt[0:8, :]), [Rp], [Rsab[b]])
                    store(cxT[:, t0:t0 + 512].rearrange("(c p) t -> p c t", p=128), stg[b][:, 0:8, :], Rstg[b], R["cxT"])
                    store(dqT[:, t0:t0 + 512].rearrange("(c p) t -> p c t", p=128), stg[b][:, 8:24, :], Rstg[b], R["dqT"])
                    store(daT[:, t0:t0 + 512], sab[b][:, 0, :], Rsab[b], R["daT"])
                    store(dbT[:, t0:t0 + 512], sab[b][:, 1, :], Rsab[b], R["dbT"])
            P.barrier()

        def phaseA2(l, xsrc, xres):
            with ExitStack() as ps:
                T_ = lambda n, s, d=F32: ps.enter_context(sbt(n, list(s), d))
                wA = T_("wA2", [128, 8, 1952], BF16); RwA = P.dma_res("wA2")
                for kc in range(8):
                    load(wA[:, kc, :], wb["w_in"][l, kc * 128:(kc + 1) * 128, 0:1952], RwA, src=[R_wb])
                wuq = T_("wuq", [128, 2, 768], BF16); wukv = T_("wukv", [128, 1024], BF16)
                load(wuq[:], wb["b_wuq"][l].rearrange("(k p) n -> p k n", p=128), RwA, src=[R_wb])
                load(wukv[:], wb["b_wukv"][l], RwA, src=[R_wb])
                gt = T_("gt", [128, D]); Rg = P.dma_res("gtA2")
                load(gt[:], W["ln1_g"][l:l + 1, :].broadcast_to([128, D]), Rg)
                gAq = T_("gAq", [128, 512]); gAk = T_("gAk", [128, 512]); gBq = T_("gBq", [128, 768]); gBk = T_("gBk", [128, 768])
                gqa = T_("gqa", [128, 256]); gkva = T_("gkva", [128, 128])
                bcast_rows(gAq, W["a_qn_g"][l], 8, 64, Rg); bcast_rows(gAk, W["a_kn_g"][l], 8, 64, Rg)
                bcast_rows(gBq, W["b_qn_g"][l], 8, 96, Rg); bcast_rows(gBk, W["b_kn_g"][l], 8, 96, Rg)
                bcast_rows(gqa, W["b_qa_g"][l], 1, 256, Rg); bcast_rows(gkva, W["b_kva_g"][l], 1, 128, Rg)
                xt = T_("xt", [128, 4, D]); Rxt = P.dma_res("xtA2")
                xnb = T_("xnb", [128, 4, D], BF16); Rxnb = Res()
                xnT = T_("xnT", [128, 8, 512], BF16); RxnT = Res()
                junk = T_("junk", [128, D]); Rjunk = Res(); st = T_("st", [128, 24]); Rst = Res()
                rA = T_("rA", [128, 4, 16]); rB = T_("rB", [128, 4, 32]); Rrope = P.dma_res("rope")
                sAq = T_("sAq", [128, 4, 512], BF16); sAk = T_("sAk", [128, 4, 512], BF16); sAv = T_("sAv", [128, 4, 512], BF16)
                sBq = T_("sBq", [128, 8, 512], BF16); sBk = T_("sBk", [128, 8, 512], BF16); sBv = T_("sBv", [128, 4, 512], BF16)
                RsAq, RsAk, RsAv, RsBq, RsBk, RsBv = [Res() for _ in range(6)]
                hsq = T_("hsq", [128, 768]); Rhsq = Res(); hn = T_("hn", [128, 768]); Rhn = Res()
                hst = T_("hst", [128, 24]); Rhst = Res(); tr = T_("tr", [128, 4, 128]); Rtr = Res()
                ob = T_("ob", [128, 768], BF16); Rob = Res()
                qf = T_("qf", [128, 768]); Rqf = Res(); kf = T_("kf", [128, 768]); Rkf = Res()
                krs = T_("krs", [128, 32]); Rkrs = Res()
                cqb = T_("cqb", [128, 384], BF16); Rcqb = Res(); cT = T_("cT", [128, 3, 128], BF16); RcT = Res()

                def head_proc(src3, rsrc, H, Dh, gain, r0, nf, cos, sin):
                    HD = H * Dh
                    v3 = lambda t: t[:, 0:HD].rearrange("p (h d) -> p h d", d=Dh)
                    ACT(lambda e: e.activation(out=v3(hsq), in_=src3, func=AF.Square), rsrc, [Rhsq])
                    DVE(lambda e: e.reduce_sum(out=hst[:, 0:H], in_=v3(hsq), axis=AX.X), [Rhsq], [Rhst])
                    ACT(lambda e: e.activation(out=hst[:, 8:8 + H], in_=hst[:, 0:H], func=AF.Sqrt, scale=1.0 / Dh, bias=EPSC[:, 0:1]), [Rhst, REPS], [Rhst])
                    DVE(lambda e: e.reciprocal(out=hst[:, 16:16 + H], in_=hst[:, 8:8 + H]), [Rhst], [Rhst])
                    DVE(lambda e: e.tensor_tensor(out=v3(hn), in0=src3, in1=bc_last(hst[:, 16:16 + H], Dh), op=ALU.mult), rsrc + [Rhst], [Rhn])
                    DVE(lambda e: e.tensor_tensor(out=hn[:, 0:HD], in0=hn[:, 0:HD], in1=gain[:, 0:HD], op=ALU.mult), [Rhn, Rg], [Rhn])
                    ACT(lambda e: e.activation(out=ob[:, 0:HD], in_=hn[:, 0:HD], func=AF.Copy), [Rhn], [Rob])
                    a = v3(hn)[:, :, r0:r0 + nf]; b_ = v3(hn)[:, :, r0 + nf:r0 + 2 * nf]
                    c = bc_mid(cos, H); s = bc_mid(sin, H)
                    tv = lambda i: tr[:, i, 0:H * nf].rearrange("p (h f) -> p h f", f=nf)
                    DVE(lambda e: e.tensor_tensor(out=tv(0), in0=a, in1=c, op=ALU.mult), [Rhn, Rrope], [Rtr])
                    DVE(lambda e: e.tensor_tensor(out=tv(1), in0=b_, in1=s, op=ALU.mult), [Rhn, Rrope], [Rtr])
                    DVE(lambda e: e.tensor_tensor(out=tv(2), in0=b_, in1=c, op=ALU.mult), [Rhn, Rrope], [Rtr])
                    DVE(lambda e: e.tensor_tensor(out=tv(3), in0=a, in1=s, op=ALU.mult), [Rhn, Rrope], [Rtr])
                    DVE(lambda e: e.tensor_tensor(out=v3(ob)[:, :, r0:r0 + nf], in0=tv(0), in1=tv(1), op=ALU.subtract), [Rtr], [Rob])
                    DVE(lambda e: e.tensor_tensor(out=v3(ob)[:, :, r0 + nf:r0 + 2 * nf], in0=tv(2), in1=tv(3), op=ALU.add), [Rtr], [Rob])

                def rms_rows(src, rsrc, n, gain, dst, col0):
                    ACT(lambda e: e.activation(out=hsq[:, 0:n], in_=src, func=AF.Square), rsrc, [Rhsq])
                    DVE(lambda e: e.reduce_sum(out=hst[:, 0:1], in_=hsq[:, 0:n], axis=AX.X), [Rhsq], [Rhst])
                    ACT(lambda e: e.activation(out=hst[:, 8:9], in_=hst[:, 0:1], func=AF.Sqrt, scale=1.0 / n, bias=EPSC[:, 0:1]), [Rhst, REPS], [Rhst])
                    DVE(lambda e: e.reciprocal(out=hst[:, 16:17], in_=hst[:, 8:9]), [Rhst], [Rhst])
                    DVE(lambda e: e.scalar_tensor_tensor(out=dst[:, col0:col0 + n], in0=src, scalar=hst[:, 16:17], in1=gain[:, 0:n], op0=ALU.mult, op1=ALU.mult),
                        rsrc + [Rhst, Rg], [Rcqb])

                for g in range(NG):
                    t0 = g * 512
                    norm_T("A2", xsrc, t0, 512, gt, Rg, xt, Rxt, xnb, Rxnb, xnT, RxnT, junk, Rjunk, st, Rst, [xres])
                    load(rA[:], ropeA[t0:t0 + 512, :].rearrange("(j p) f -> p j f", p=128), Rrope)
                    load(rB[:], ropeB[t0:t0 + 512, :].rearrange("(j p) f -> p j f", p=128), Rrope)
                    for j in range(4):
                        tsl = slice(j * 128, (j + 1) * 128)
                        def proj(c0, n):
                            pt, Rp = nps()
                            for kc in range(8):
                                PE(lambda e, kc=kc, pt=pt: e.matmul(pt[:, 0:n], xnT[:, kc, tsl], wA[:, kc, c0:c0 + n], start=(kc == 0), stop=(kc == 7)), [RwA, RxnT], [Rp])
                            return pt, Rp
                        for (c0, gain, sdst, Rs) in (((0, gAq, sAq, RsAq), (512, gAk, sAk, RsAk)) if 'Aqk' not in _SK else ()):
                            pt, Rp = proj(c0, 512)
                            head_proc(pt[:, :].rearrange("p (h d) -> p h d", d=64), [Rp], 8, 64, gain, 0, 8, rA[:, j, 0:8], rA[:, j, 8:16])
                            pb, Rpb = npsb()
                            for pr in range(4):
                                PE(lambda e, pr=pr, pb=pb: e.transpose(out=pb[:, pr * 128:(pr + 1) * 128], in_=ob[:, pr * 128:(pr + 1) * 128], identity=identb[:]), [Rob, Ridb], [Rpb])
                            ACT(lambda e, pb=pb, sdst=sdst: e.activation(out=sdst[:, :, tsl], in_=pb[:, 0:512].rearrange("p (a t) -> p a t", t=128), func=AF.Copy), [Rpb], [Rs])
                        pt, Rp = proj(1024, 512)
                        ACT(lambda e, pt=pt: e.activation(out=sAv[:, j, :], in_=pt[:, :], func=AF.Copy), [Rp], [RsAv])
                        if 'B' in _SK: continue
                        ptB, RpB = proj(1536, 416)
                        rms_rows(ptB[:, 0:256], [RpB], 256, gqa, cqb, 0)
                        rms_rows(ptB[:, 256:384], [RpB], 128, gkva, cqb, 256)
                        pb, Rpb = npsb()
                        for i3 in range(3):
                            PE(lambda e, i3=i3, pb=pb: e.transpose(out=pb[:, i3 * 128:(i3 + 1) * 128], in_=cqb[:, i3 * 128:(i3 + 1) * 128], identity=identb[:]), [Rcqb, Ridb], [Rpb])
                        DVE(lambda e, pb=pb: e.tensor_copy(out=cT[:], in_=pb[:, 0:384].rearrange("p (a t) -> p a t", t=128)), [Rpb], [RcT])
                        if 'Bq' in _SK: continue
                        for (c0, n) in ((0, 512), (512, 256)):
                            pt, Rp = nps()
                            for kc in range(2):
                                PE(lambda e, kc=kc, pt=pt, c0=c0, n=n: e.matmul(pt[:, 0:n], cT[:, kc, :], wuq[:, kc, c0:c0 + n], start=(kc == 0), stop=(kc == 1)), [RcT, RwA], [Rp])
                            DVE(lambda e, pt=pt, c0=c0, n=n: e.tensor_copy(out=qf[:, c0:c0 + n], in_=pt[:, 0:n]), [Rp], [Rqf])
                        head_proc(qf[:, :].rearrange("p (h d) -> p h d", d=96), [Rqf], 8, 96, gBq, 64, 16, rB[:, j, 0:16], rB[:, j, 16:32])
                        pb, Rpb = npsb()
                        for h in range(8):
                            PE(lambda e, h=h, pb=pb: e.transpose(out=pb[0:96, h * 128:(h + 1) * 128], in_=ob[:, h * 96:(h + 1) * 96], identity=identb[:]), [Rob, Ridb], [Rpb])
                        ACT(lambda e, pb=pb: e.activation(out=sBq[0:96, :, tsl], in_=pb[0:96, :].rearrange("p (a t) -> p a t", t=128), func=AF.Copy), [Rpb], [RsBq])
                        if 'Bkv' in _SK: continue
                        kf3 = kf[:, :].rearrange("p (h d) -> p h d", d=96)
                        for half in range(2):
                            pt, Rp = nps()
                            PE(lambda e, pt=pt, half=half: e.matmul(pt[:, :], cT[:, 2, :], wukv[:, half * 512:(half + 1) * 512], start=True, stop=True), [RcT, RwA], [Rp])
                            kv3 = pt[:, :].rearrange("p (h d) -> p h d", d=128)
                            DVE(lambda e, kv3=kv3, half=half: e.tensor_copy(out=kf3[:, half * 4:(half + 1) * 4, 0:64], in_=kv3[:, :, 0:64]), [Rp], [Rkf])
                            ACT(lambda e, kv3=kv3, half=half: e.activation(out=sBv[:, j, half * 256:(half + 1) * 256].rearrange("p (h d) -> p h d", d=64), in_=kv3[:, :, 64:128], func=AF.Copy), [Rp], [RsBv])
                        DVE(lambda e, ptB=ptB: e.tensor_copy(out=krs[:], in_=ptB[:, 384:416]), [RpB], [Rkrs])
                        DVE(lambda e: e.tensor_copy(out=kf3[:, :, 64:96], in_=bc_mid(krs[:, :], 8)), [Rkrs], [Rkf])
                        head_proc(kf3, [Rkf], 8, 96, gBk, 64, 16, rB[:, j, 0:16], rB[:, j, 16:32])
                        pb, Rpb = npsb()
                        for h in range(8):
                            PE(lambda e, h=h, pb=pb: e.transpose(out=pb[0:96, h * 128:(h + 1) * 128], in_=ob[:, h * 96:(h + 1) * 96], identity=identb[:]), [Rob, Ridb], [Rpb])
                        ACT(lambda e, pb=pb: e.activation(out=sBk[0:96, :, tsl], in_=pb[0:96, :].rearrange("p (a t) -> p a t", t=128), func=AF.Copy), [Rpb], [RsBk])
                    if 'st' in _SK: continue
                    store(aqT[:, :, t0:t0 + 512].rearrange("a p t -> p a t"), sAq[:], RsAq, R["aqT"])
                    store(akT[:, :, t0:t0 + 512].rearrange("a p t -> p a t"), sAk[:], RsAk, R["akT"])
                    store(av[t0:t0 + 512, :].rearrange("(j p) c -> p j c", p=128), sAv[:], RsAv, R["av"])
                    store(bqT[:, :, t0:t0 + 512].rearrange("h p t -> p h t"), sBq[0:96, :, :], RsBq, R["bqT"])
                    store(bkT[:, :, t0:t0 + 512].rearrange("h p t -> p h t"), sBk[0:96, :, :], RsBk, R["bkT"])
                    store(bv[t0:t0 + 512, :].rearrange("(j p) c -> p j c", p=128), sBv[:], RsBv, R["bv"])
            P.barrier()

        def attention(tag, qsrc, ksrc, vsrc, Rq, Rk, Rv, dk, scale, orow0, windowed):
            with ExitStack() as ps:
                T_ = lambda n, s, d=F32: ps.enter_context(sbt(n, list(s), d))
                KT = T_("KT", [128, T], BF16); RKT = P.dma_res("KT" + tag)
                Vt = T_("Vt", [128, NT, 128], BF16); RVt = P.dma_res("Vt" + tag)
                DVE(lambda e: e.memset(Vt[:], 1.0), [], [RVt])
                QT = [T_("QT%d" % i, [128, 512], BF16) for i in range(2)]; RQT = [P.dma_res("QT%d%s" % (i, tag)) for i in range(2)]
                pT = [T_("pT%d" % i, [128, 512], BF16) for i in range(6)]; RpT = [Res() for _ in range(6)]
                ev = [T_("ev%d" % i, [128, 512]) for i in range(2)]; Rev = [Res(), Res()]; rcp = T_("rcp", [64, 512]); Rrcp = Res()
                ostg = [T_("ostg%d" % i, [64, 512], BF16) for i in range(2)]; Rostg = [Res(), Res()]
                if windowed:
                    am32 = T_("am32", [128, 512]); Ram32 = P.dma_res("am32")
                    amk = T_("amk", [128, 20, 512], BF16); Ramk = Res()
                    for m in range(20):
                        load(am32[:], amask_in[m], Ram32)
                        DVE(lambda e, m=m: e.tensor_copy(out=amk[:, m, :], in_=am32[:]), [Ram32], [Ramk])
                LA = 3
                it = [0]
                def ring():
                    si = it[0] % 4; it[0] += 1; return si
                pTi = [0]
                for h in range(8):
                    load(KT[0:dk, :], ksrc(h), RKT, src=[Rk])
                    vsr = vsrc(h).rearrange("(n p) c -> p n c", p=128)
                    for n0_ in range(0, NT, 16):
                        load(Vt[:, n0_:n0_ + 16, 0:64], vsr[:, n0_:n0_ + 16, :], RVt, src=[Rv])
                    items = []
                    for g in range(NG):
                        q0 = g * 512
                        if windowed:
                            k_lo = max(0, (q0 - 1024) // 128); k_hi = min(NT, (q0 + 512 + 1024) // 128)
                        else:
                            k_lo, k_hi = 0, NT
                        for kt in range(k_lo, k_hi):
                            items.append((g, kt, kt == k_lo, kt == k_hi - 1))
                    n_it = len(items)
                    load(QT[0][0:dk, :], qsrc(h)[:, 0:512], RQT[0], src=[Rq])
                    slot = {}
                    pend = []
                    for step in range(n_it + LA + 3):
                        if step < n_it:
                            g, kt, first, last = items[step]
                            q0 = g * 512; qb = g % 2
                            if first and g + 1 < NG:
                                load(QT[1 - qb][0:dk, :], qsrc(h)[:, q0 + 512:q0 + 1024], RQT[1 - qb], src=[Rq])
                            si = ring(); pi = pTi[0] % 6; pTi[0] += 1
                            slot[step] = pi
                            pS, RpS = psf[si], Rpsf[si]
                            PE(lambda e, pS=pS, kt=kt, qb=qb: e.matmul(pS[:, :], KT[0:dk, kt * 128:(kt + 1) * 128], QT[qb][0:dk, :], start=True, stop=True),
                               [RKT, RQT[qb]], [RpS])
                            cross = ((kt * 128) // HALF) != (q0 // HALF)
                            bcol = cfg[:, 1:2] if cross else cfg[:, 0:1]
                            ACT(lambda e, pS=pS, pi=pi, bcol=bcol: e.activation(out=pT[pi][:], in_=pS[:, :], func=AF.Exp, bias=bcol, scale=scale), [RpS, Rcfg], [RpT[pi]])
                            if windowed:
                                m = (kt * 128 - q0 + 1024) // 128
                                DVE(lambda e, pi=pi, m=m: e.tensor_tensor(out=pT[pi][:], in0=pT[pi][:], in1=amk[:, m, :], op=ALU.mult), [RpT[pi], Ramk], [RpT[pi]])
                        j = step - LA
                        if 0 <= j < n_it:
                            g, kt, first, last = items[j]
                            q0 = g * 512; qb = g % 2; pi = slot.pop(j)
                            po, Rpo = psf[4 + qb], Rpsf[4 + qb]
                            PE(lambda e, po=po, kt=kt, pi=pi, first=first, last=last: e.matmul(po[:, :], Vt[:, kt, :], pT[pi][:], start=first, stop=last),
                               [RVt, RpT[pi]], [Rpo])
                            if last:
                                DVE(lambda e, po=po, qb=qb: e.tensor_copy(out=ev[qb][:], in_=po[:, :]), [Rpo], [Rev[qb]])
                                def fin(qb=qb, q0=q0):
                                    si = ring()
                                    pd, Rpd = psf[si], Rpsf[si]
                                    PE(lambda e, pd=pd: e.matmul(pd[0:64, :], cst[:, 448:512], ev[qb][:], start=True, stop=True), [Rev[qb], Rcst], [Rpd])
                                    DVE(lambda e, pd=pd: e.reciprocal(out=rcp[:], in_=pd[0:64, :]), [Rpd], [Rrcp])
                                    DVE(lambda e: e.tensor_tensor(out=ostg[qb][:], in0=ev[qb][0:64, :], in1=rcp[:], op=ALU.mult), [Rev[qb], Rrcp], [Rostg[qb]])
                                    store(oT[orow0 + h * 64:orow0 + (h + 1) * 64, q0:q0 + 512], ostg[qb][:], Rostg[qb], R["oT"])
                                pend.append((step + 3, fin))
                        while pend and pend[0][0] <= step:
                            pend.pop(0)[1]()
                    assert not pend
            P.barrier()

        def phaseC_rglru(l):
            SEG = T // 4; NB = SEG // 512
            with ExitStack() as ps:
                T_ = lambda n, s, d=F32: ps.enter_context(sbt(n, list(s), d))
                cols = T_("rcols", [128, 4, 16]); Rcols = P.dma_res("rcols")
                colap = lambda a: a.rearrange("(p o) -> p o", o=1)
                for c in range(4):
                    cs = slice(c * 128, (c + 1) * 128)
                    for j in range(4):
                        load(cols[:, c, j:j + 1], colap(W["c_conv_w"][l, j, cs]), Rcols)
                    load(cols[:, c, 4:5], colap(W["c_conv_b"][l, cs]), Rcols)
                    for d in range(2):
                        load(cols[:, c, 5 + d:6 + d], colap(W["c_br"][l, d, cs]), Rcols)
                        load(cols[:, c, 7 + d:8 + d], colap(W["c_bi"][l, d, cs]), Rcols)
                        load(cols[:, c, 9 + d:10 + d], colap(W["c_lam"][l, d, cs]), Rcols)
                ACT(lambda e: e.activation(out=cols[:, :, 11:13], in_=cols[:, :, 9:11], func=AF.Exp, scale=-1.0), [Rcols], [Rcols])
                ACT(lambda e: e.activation(out=cols[:, :, 11:13], in_=cols[:, :, 11:13], func=AF.Ln, bias=1.0), [Rcols], [Rcols])
                DVE(lambda e: e.tensor_scalar(out=cols[:, :, 13:15], in0=cols[:, :, 11:13], scalar1=-16.0, scalar2=None, op0=ALU.mult), [Rcols], [Rcols])
                DVE(lambda e: e.tensor_scalar(out=cols[:, :, 11:13], in0=cols[:, :, 11:13], scalar1=-8.0, scalar2=None, op0=ALU.mult), [Rcols], [Rcols])
                w32 = T_("w32", [128, 16, 128]); Rw32 = P.dma_res("w32"); wbd = T_("wbd", [128, 16, 128], BF16); Rwbd = Res()
                DVE(lambda e: e.memset(w32[:], 0.0), [], [Rw32])
                for d in range(2):
                    for gi_, nm in enumerate(("c_wr", "c_wi")):
                        for c in range(4):
                            idx = (d * 2 + gi_) * 4 + c
                            load(w32[0:64, idx, 0:64], W[nm][l, d, 2 * c], Rw32)
                            load(w32[64:128, idx, 64:128], W[nm][l, d, 2 * c + 1], Rw32)
                DVE(lambda e: e.tensor_copy(out=wbd[:], in_=w32[:]), [Rw32], [Rwbd])
                xin = T_("xin", [128, SEG + 4], BF16); Rxin = P.dma_res("xin")
                xc = T_("xc", [128, SEG]); Rxc = Res(); xcb = T_("xcb", [128, SEG], BF16); Rxcb = Res()
                tA = T_("tA", [128, SEG]); RtA = P.dma_res("tA"); tB = T_("tB", [128, SEG]); RtB = Res(); tC = T_("tC", [128, SEG]); RtC = Res()
                cg = T_("cg", [128, SEG], BF16); Rcg = P.dma_res("cg"); ot = T_("ot", [128, SEG], BF16); Rot = Res()
                car = T_("car", [128, 8]); Rcar = Res()
                def seg_common(c, s, d):
                    t0 = s * SEG; r0 = c * 128
                    DVE(lambda e: e.memset(xin[:], 0.0), [], [Rxin])
                    lo = max(0, t0 - 2); hi = min(T, t0 + SEG + 1)
                    load(xin[:, 2 - (t0 - lo):2 + (hi - t0)], cxT[r0:r0 + 128, lo:hi], Rxin, src=[R["cxT"]])
                    if s == 2:
                        DVE(lambda e: e.tensor_scalar(out=xin[:, 0:2], in0=xin[:, 0:2], scalar1=cfg[:, 2:3], scalar2=None, op0=ALU.mult), [Rxin, Rcfg], [Rxin])
                    if s == 1:
                        DVE(lambda e: e.tensor_scalar(out=xin[:, SEG + 2:SEG + 3], in0=xin[:, SEG + 2:SEG + 3], scalar1=cfg[:, 2:3], scalar2=None, op0=ALU.mult), [Rxin, Rcfg], [Rxin])
                    DVE(lambda e: e.tensor_scalar(out=xc[:], in0=xin[:, 0:SEG], scalar1=cols[:, c, 0:1], scalar2=cols[:, c, 4:5], op0=ALU.mult, op1=ALU.add), [Rxin, Rcols], [Rxc])
                    for j in range(1, 4):
                        DVE(lambda e, j=j: e.scalar_tensor_tensor(out=xc[:], in0=xin[:, j:j + SEG], scalar=cols[:, c, j:j + 1], in1=xc[:], op0=ALU.mult, op1=ALU.add), [Rxin, Rcols, Rxc], [Rxc])
                    ACT(lambda e: e.activation(out=xcb[:], in_=xc[:], func=AF.Copy), [Rxc], [Rxcb])
                    for gi_, (dst, Rd, bcol) in enumerate(((tA, RtA, 5 + d), (tB, RtB, 7 + d))):
                        idx = (d * 2 + gi_) * 4 + c
                        for b in range(NB):
                            pt, Rp = nps()
                            PE(lambda e, pt=pt, b=b, idx=idx: e.matmul(pt[:, :], wbd[:, idx, :], xcb[:, b * 512:(b + 1) * 512], start=True, stop=True), [Rwbd, Rxcb], [Rp])
                            ACT(lambda e, pt=pt, b=b, dst=dst, bcol=bcol: e.activation(out=dst[:, b * 512:(b + 1) * 512], in_=pt[:, :], func=AF.Sigmoid, bias=cols[:, c, bcol:bcol + 1]), [Rp, Rcols], [Rd])
                    ACT(lambda e: e.activation(out=tC[:], in_=tA[:], func=AF.Exp, scale=cols[:, c, 13 + d:14 + d]), [RtA, Rcols], [RtC])
                    ACT(lambda e: e.activation(out=tC[:], in_=tC[:], func=AF.Sqrt, scale=-1.0, bias=ONEC[:, 0:1]), [RtC, RONE], [RtC])
                    ACT(lambda e: e.activation(out=tA[:], in_=tA[:], func=AF.Exp, scale=cols[:, c, 11 + d:12 + d]), [RtA, Rcols], [RtA])
                    DVE(lambda e: e.tensor_tensor(out=tB[:], in0=tB[:], in1=tC[:], op=ALU.mult), [RtB, RtC], [RtB])
                    DVE(lambda e: e.tensor_tensor(out=tB[:], in0=tB[:], in1=xc[:], op=ALU.mult), [RtB, Rxc], [RtB])
                    if d == 0 and s == 2:
                        DVE(lambda e: e.tensor_scalar(out=tA[:, 0:1], in0=tA[:, 0:1], scalar1=cfg[:, 2:3], scalar2=None, op0=ALU.mult), [RtA, Rcfg], [RtA])
                    if d == 1 and s == 1:
                        DVE(lambda e: e.tensor_scalar(out=tA[:, SEG - 1:SEG], in0=tA[:, SEG - 1:SEG], scalar1=cfg[:, 2:3], scalar2=None, op0=ALU.mult), [RtA, Rcfg], [RtA])
                    first = (s == 0) if d == 0 else (s == 3)
                    init = 0.0 if first else car[:, c * 2 + d:c * 2 + d + 1]
                    if d == 0:
                        DVE(lambda e: e.tensor_tensor_scan(out=tC[:], data0=tA[:], data1=tB[:], initial=init, op0=ALU.mult, op1=ALU.add), [RtA, RtB, Rcar], [RtC])
                        DVE(lambda e: e.tensor_copy(out=car[:, c * 2:c * 2 + 1], in_=tC[:, SEG - 1:SEG]), [RtC], [Rcar])
                    else:
                        DVE(lambda e: e.tensor_tensor_scan(out=rev_ap(tC[:]), data0=rev_ap(tA[:]), data1=rev_ap(tB[:]), initial=init, op0=ALU.mult, op1=ALU.add), [RtA, RtB, Rcar], [RtC])
                        DVE(lambda e: e.tensor_copy(out=car[:, c * 2 + 1:c * 2 + 2], in_=tC[:, 0:1]), [RtC], [Rcar])
                for c in range(4):
                    for s in range(4):
                        seg_common(c, s, 0)
                        store(hfT[c * 128:(c + 1) * 128, s * SEG:(s + 1) * SEG], tC[:], RtC, R["hfT"])
                for c in range(4):
                    for s in (3, 2, 1, 0):
                        seg_common(c, s, 1)
                        sl = slice(s * SEG, (s + 1) * SEG)
                        load(tA[:], hfT[c * 128:(c + 1) * 128, sl], RtA, src=[R["hfT"]])
                        load(cg[:], cxT[512 + c * 128:512 + (c + 1) * 128, sl], Rcg, src=[R["cxT"]])
                        DVE(lambda e: e.tensor_tensor(out=tC[:], in0=tC[:], in1=tA[:], op=ALU.add), [RtC, RtA], [RtC])
                        ACT(lambda e: e.activation(out=tA[:], in_=cg[:], func=AF.Gelu_apprx_tanh), [Rcg, RtA], [RtA])
                        DVE(lambda e: e.tensor_tensor(out=ot[:], in0=tC[:], in1=tA[:], op=ALU.mult), [RtC, RtA], [Rot])
                        store(oT[1024 + c * 128:1024 + (c + 1) * 128, sl], ot[:], Rot, R["oT"])
            P.barrier()

        ONEC = S("onec", [128, 1]); RONE = Res()
        DVE(lambda e: e.memset(ONEC[:], 1.0), [], [RONE])

        def phaseD_gdn(l):
            BLK = 4096 if T >= 4096 else T
            with ExitStack() as ps:
                T_ = lambda n, s, d=F32: ps.enter_context(sbt(n, list(s), d))
                c8 = T_("c8", [8, 4]); Rc8 = P.dma_res("c8")
                load(c8[:, 0:1], W["d_a_log"][l].rearrange("d (h o) -> (d h) o", o=1), Rc8)
                load(c8[:, 1:2], W["d_dt_bias"][l].rearrange("d (h o) -> (d h) o", o=1), Rc8)
                ACT(lambda e: e.activation(out=c8[:, 2:3], in_=c8[:, 0:1], func=AF.Exp), [Rc8], [Rc8])
                DVE(lambda e: e.tensor_scalar(out=c8[:, 2:3], in0=c8[:, 2:3], scalar1=-1.0, scalar2=None, op0=ALU.mult), [Rc8], [Rc8])
                mF = T_("mF", [8, BLK]); mB = T_("mB", [8, BLK]); Rm = Res()
                DVE(lambda e: e.memset(mF[:], 1.0), [], [Rm]); DVE(lambda e: e.memset(mB[:], 1.0), [], [Rm])
                DVE(lambda e: e.memset(mF[:].rearrange("p (n j) -> p n j", j=64)[:, :, 0:1], 0.0), [], [Rm])
                DVE(lambda e: e.memset(mB[:].rearrange("p (n j) -> p n j", j=64)[:, :, 63:64], 0.0), [], [Rm])
                ga = T_("ga", [8, BLK]); Rga = P.dma_res("ga"); gb = T_("gb", [8, BLK]); Rgb = P.dma_res("gb")
                gp = T_("gp", [8, BLK]); Rgp = Res(); gs = T_("gs", [8, BLK]); Rgs = Res()
                for b0 in range(0, T, BLK):
                    sl = slice(b0, b0 + BLK)
                    load(ga[:], daT[:, sl], Rga, src=[R["daT"]]); load(gb[:], dbT[:, sl], Rgb, src=[R["dbT"]])
                    ACT(lambda e: e.activation(out=ga[:], in_=ga[:], func=AF.Exp, bias=c8[:, 1:2]), [Rga, Rc8], [Rga])
                    ACT(lambda e: e.activation(out=ga[:], in_=ga[:], func=AF.Ln, bias=1.0), [Rga], [Rga])
                    DVE(lambda e: e.tensor_scalar(out=ga[:], in0=ga[:], scalar1=c8[:, 2:3], scalar2=None, op0=ALU.mult), [Rga, Rc8], [Rga])
                    DVE(lambda e: e.tensor_tensor_scan(out=gp[:], data0=mF[:], data1=ga[:], initial=0.0, op0=ALU.mult, op1=ALU.add), [Rga, Rm], [Rgp])
                    DVE(lambda e: e.tensor_tensor_scan(out=rev_ap(gs[:]), data0=rev_ap(mB[:]), data1=rev_ap(ga[:]), initial=0.0, op0=ALU.mult, op1=ALU.add), [Rga, Rm], [Rgs])
                    ACT(lambda e: e.activation(out=gb[:], in_=gb[:], func=AF.Sigmoid), [Rgb], [Rgb])
                    store(gcD[0:4, sl], gp[0:4, :], Rgp, R["gcD"]); store(gcD[4:8, sl], gs[4:8, :], Rgs, R["gcD"])
                    store(gcD[8:16, sl], gb[:], Rgb, R["gcD"])
            P.barrier()
            with ExitStack() as ps:
                T_ = lambda n, s, d=F32: ps.enter_context(sbt(n, list(s), d))
                Gc = T_("Gc", [64, 16, NCH]); RGc = Res()
                gl = T_("gl", [128, 64]); Rgl = P.dma_res("gl")
                for r in range(16):
                    for n0 in range(0, NCH, 128):
                        nn = min(128, NCH - n0)
                        load(gl[0:nn, :], gcD[r, n0 * 64:(n0 + nn) * 64].rearrange("(n j) -> n j", j=64), Rgl, src=[R["gcD"]])
                        pt, Rp = nps()
                        PE(lambda e, pt=pt, nn=nn: e.transpose(out=pt[0:64, 0:nn], in_=gl[0:nn, :], identity=cst[0:nn, 0:nn]), [Rgl, Rcst], [Rp])
                        DVE(lambda e, pt=pt, nn=nn, r=r, n0=n0: e.tensor_copy(out=Gc[:, r, n0:n0 + nn], in_=pt[0:64, 0:nn]), [Rp], [RGc])
                dcol = T_("dcol", [128, 4, 12]); Rdcol = P.dma_res("dcol"); ong = T_("ong", [128, 1])
                colap = lambda a: a.rearrange("(p o) -> p o", o=1)
                for h in range(4):
                    for part in range(3):
                        for j in range(4):
                            load(dcol[:, h, part * 4 + j:part * 4 + j + 1], colap(W["d_conv_w"][l, j, part * 512 + h * 128:part * 512 + (h + 1) * 128]), Rdcol)
                load(ong[:], colap(W["d_on_g"][l]), Rdcol)
                negm = T_("negm", [64, 128]); Rnegm = Res()
                DVE(lambda e: e.tensor_scalar(out=negm[:, 0:64], in0=cst[0:64, 128:192], scalar1=-1.0, scalar2=None, op0=ALU.mult), [Rcst], [Rnegm])
                DVE(lambda e: e.tensor_scalar(out=negm[:, 64:128], in0=cst[0:64, 256:320], scalar1=-1.0, scalar2=None, op0=ALU.mult), [Rcst], [Rnegm])
                I64 = cst[0:64, 384:448]
                S32 = [T_("S32_%d" % h, [128, 128]) for h in range(4)]; Sb = [T_("Sb_%d" % h, [128, 128], BF16) for h in range(4)]
                RS32 = [Res() for _ in range(4)]; RSb = [Res() for _ in range(4)]
                qin = T_("qin", [128, 3, 516], BF16); Rqin = P.dma_res("qin")
                cv = T_("cv", [128, 3, 512]); Rcv = Res(); sqb = T_("sqb", [128, 1024], BF16); Rsqb = Res()
                rqk = T_("rqk", [128, 1024]); Rrqk = Res()
                Grow = T_("Grow", [128, 512]); RGrow = P.dma_res("Grow"); Brow = T_("Brow", [64, 512]); RBrow = P.dma_res("Brow")
                eG = T_("eG", [128, 512]); ReG = Res(); qtb = T_("qtb", [128, 512], BF16); Rqtb = Res()
                vtb = T_("vtb", [128, 512], BF16); Rvtb = Res()
                dl = T_("dl", [64, 512]); Rdl = Res(); e1 = T_("e1", [64, 512]); Re1 = Res(); e2 = T_("e2", [64, 512]); Re2 = Res()
                DL = T_("DL", [64, 512]); RDL = Res(); DLT = T_("DLT", [64, 512]); RDLT = Res(); DA = T_("DA", [64, 512]); RDA = Res()
                XH = [[T_("X%d_%d" % (h_, i), [64, 512], BF16) for i in range(2)] for h_ in range(4)]
                YH = [[T_("Y%d_%d" % (h_, i), [64, 512], BF16) for i in range(2)] for h_ in range(4)]
                QH = [[T_("Qc%d_%d" % (h_, i), [64, 512], BF16) for i in range(2)] for h_ in range(4)]
                RXH = [[Res(), Res()] for _ in range(4)]; RYH = [[Res(), Res()] for _ in range(4)]; RQH = [[Res(), Res()] for _ in range(4)]
                sm = T_("sm", [64, 32]); Rsm = Res()
                Kt = [T_("Kt%d" % h, [128, 512], BF16) for h in range(4)]; Qg = [T_("Qg%d" % h, [128, 512], BF16) for h in range(4)]
                aT = [T_("aT%d" % h, [64, 512], BF16) for h in range(4)]; Qf = [T_("Qf%d" % h, [64, 512], BF16) for h in range(4)]
                Ktil = [T_("Ktil%d" % h, [64, 8, 128], BF16) for h in range(4)]; Vb = [T_("Vb%d" % h, [64, 8, 128]) for h in range(4)]
                cwc = [T_("cw%d" % h, [64, 8]) for h in range(4)]; El = [T_("El%d" % h, [128, 8]) for h in range(4)]
                RKt, RQg, RaT, RQf, RKtil, RVb, Rcw, REl = [[Res() for _ in range(4)] for _ in range(8)]
                Rm_ = [T_("Rm%d" % h, [64, 128], BF16) for h in range(4)]; RRm = [Res() for _ in range(4)]
                vn = [T_("vn%d" % h, [64, 128], BF16) for h in range(4)]; Rvn = [Res() for _ in range(4)]
                ofs = T_("ofs", [128, 512]); Rofs = P.dma_res("ofs"); zt = T_("zt", [128, 512], BF16); Rzt = P.dma_res("zt")
                osum = T_("osum", [128, 512]); Rosum = Res(); ostg = T_("ostgD", [128, 512], BF16); Rostg = Res()
                r2 = [0]
                def ring2():
                    i = r2[0] % 2; r2[0] += 1; return psf[i], Rpsf[i]
                v8 = lambda t: t[:, :].rearrange("p (n j) -> p n j", j=64)
                for d in range(2):
                    LAST = 63 if d == 0 else 0
                    mL = negm[:, 0:64] if d == 0 else negm[:, 64:128]
                    mLT = negm[:, 64:128] if d == 0 else negm[:, 0:64]
                    mA = cst[0:64, 320:384] if d == 0 else cst[0:64, 192:256]
                    for h in range(4):
                        DVE(lambda e, h=h: e.memset(S32[h][:], 0.0), [], [RS32[h]])
                        DVE(lambda e, h=h: e.memset(Sb[h][:], 0.0), [], [RSb[h]])
                    blocks = range(NG) if d == 0 else range(NG - 1, -1, -1)
                    for b in blocks:
                        t0 = b * 512; n0 = b * 8
                        for h in range(4):
                            r = d * 4 + h
                            X, Y, Qc, RX, RY, RQc = XH[h], YH[h], QH[h], RXH[h], RYH[h], RQH[h]
                            DVE(lambda e: e.memset(qin[:], 0.0), [], [Rqin])
                            lo = max(0, t0 - 2); hi = min(T, t0 + 513)
                            for part in range(3):
                                load(qin[:, part, 2 - (t0 - lo):2 + (hi - t0)], dqT[part * 512 + h * 128:part * 512 + (h + 1) * 128, lo:hi], Rqin, src=[R["dqT"]])
                            if t0 == HALF:
                                DVE(lambda e: e.tensor_scalar(out=qin[:, :, 0:2], in0=qin[:, :, 0:2], scalar1=cfg[:, 2:3], scalar2=None, op0=ALU.mult), [Rqin, Rcfg], [Rqin])
                            if t0 + 512 == HALF:
                                DVE(lambda e: e.tensor_scalar(out=qin[:, :, 514:515], in0=qin[:, :, 514:515], scalar1=cfg[:, 2:3], scalar2=None, op0=ALU.mult), [Rqin, Rcfg], [Rqin])
                            for part in range(3):
                                DVE(lambda e, part=part, h=h: e.tensor_scalar(out=cv[:, part, :], in0=qin[:, part, 0:512], scalar1=dcol[:, h, part * 4:part * 4 + 1], scalar2=None, op0=ALU.mult), [Rqin, Rdcol], [Rcv])
                                for j in range(1, 4):
                                    DVE(lambda e, part=part, h=h, j=j: e.scalar_tensor_tensor(out=cv[:, part, :], in0=qin[:, part, j:j + 512], scalar=dcol[:, h, part * 4 + j:part * 4 + j + 1], in1=cv[:, part, :], op0=ALU.mult, op1=ALU.add), [Rqin, Rdcol, Rcv], [Rcv])
                            ACT(lambda e: e.activation(out=cv[:], in_=cv[:], func=AF.Silu), [Rcv], [Rcv])
                            ACT(lambda e: e.activation(out=sqb[:].rearrange("p (a t) -> p a t", t=512), in_=cv[:, 0:2, :], func=AF.Square), [Rcv], [Rsqb])
                            for a_ in range(2):
                                pt, Rp = ring2()
                                PE(lambda e, pt=pt, a_=a_: e.matmul(pt[:, :], onesb[:], sqb[:, a_ * 512:(a_ + 1) * 512], start=True, stop=True), [Rones, Rsqb], [Rp])
                                ACT(lambda e, pt=pt, a_=a_: e.activation(out=rqk[:, a_ * 512:(a_ + 1) * 512], in_=pt[:, :], func=AF.Sqrt, bias=EPSC[:, 0:1]), [Rp, REPS], [Rrqk])
                            DVE(lambda e: e.reciprocal(out=rqk[:], in_=rqk[:]), [Rrqk], [Rrqk])
                            DVE(lambda e, h=h: e.tensor_tensor(out=Kt[h][:], in0=cv[:, 1, :], in1=rqk[:, 512:1024], op=ALU.mult), [Rcv, Rrqk], [RKt[h]])
                            DVE(lambda e: e.scalar_tensor_tensor(out=cv[:, 0, :], in0=cv[:, 0, :], scalar=128.0 ** -0.5, in1=rqk[:, 0:512], op0=ALU.mult, op1=ALU.mult), [Rcv, Rrqk], [Rcv])
                            ACT(lambda e: e.activation(out=qtb[:], in_=cv[:, 0, :], func=AF.Copy), [Rcv], [Rqtb])
                            ACT(lambda e: e.activation(out=vtb[:], in_=cv[:, 2, :], func=AF.Copy), [Rcv], [Rvtb])
                            load(Grow[:], gcD[r:r + 1, t0:t0 + 512].broadcast_to([128, 512]), RGrow, src=[R["gcD"]])
                            load(Brow[:], gcD[8 + r:9 + r, t0:t0 + 512].broadcast_to([64, 512]), RBrow, src=[R["gcD"]])
                            gcol = Gc[:, r, n0:n0 + 8]; bcol_ = Gc[:, 8 + r, n0:n0 + 8]
                            ACT(lambda e: e.activation(out=eG[:], in_=Grow[:], func=AF.Exp), [RGrow], [ReG])
                            DVE(lambda e, h=h: e.tensor_tensor(out=Qg[h][:], in0=cv[:, 0, :], in1=eG[:], op=ALU.mult), [Rcv, ReG], [RQg[h]])
                            ACT(lambda e, h=h: e.activation(out=El[h][:], in_=v8(Grow)[:, :, LAST], func=AF.Exp), [RGrow], [REl[h]])
                            DVE(lambda e: e.tensor_tensor(out=v8(dl), in0=v8(Grow)[0:64], in1=bc_last(gcol, 64), op=ALU.subtract), [RGrow, RGc], [Rdl])
                            DVE(lambda e: e.tensor_scalar(out=e2[:], in0=dl[:], scalar1=0.0, scalar2=None, op0=ALU.min), [Rdl], [Re2])
                            DVE(lambda e: e.tensor_scalar(out=e1[:], in0=dl[:], scalar1=0.0, scalar2=None, op0=ALU.max), [Rdl], [Re1])
                            ACT(lambda e: e.activation(out=e2[:], in_=e2[:], func=AF.Exp), [Re2], [Re2])
                            ACT(lambda e: e.activation(out=e1[:], in_=e1[:], func=AF.Exp, scale=-1.0), [Re1], [Re1])
                            DVE(lambda e: e.tensor_tensor(out=v8(DL), in0=v8(e1), in1=bc_mid(mL, 8), op=ALU.mult), [Re1, Rnegm], [RDL])
                            DVE(lambda e: e.tensor_tensor(out=v8(DL), in0=v8(DL), in1=bc_last(bcol_, 64), op=ALU.mult), [RDL, RGc], [RDL])
                            DVE(lambda e: e.tensor_tensor(out=v8(DLT), in0=v8(e2), in1=bc_mid(mLT, 8), op=ALU.mult), [Re2, Rnegm], [RDLT])
                            DVE(lambda e: e.tensor_tensor(out=DLT[:], in0=DLT[:], in1=Brow[:], op=ALU.mult), [RDLT, RBrow], [RDLT])
                            DVE(lambda e: e.tensor_tensor(out=v8(DA), in0=v8(e2), in1=bc_mid(mA, 8), op=ALU.mult), [Re2, Rcst], [RDA])
                            ACT(lambda e: e.activation(out=sm[:, 0:8], in_=gcol, func=AF.Exp), [RGc], [Rsm])
                            DVE(lambda e, h=h: e.scalar_tensor_tensor(out=cwc[h][:], in0=sm[:, 0:8], scalar=-1.0, in1=bcol_, op0=ALU.mult, op1=ALU.mult), [Rsm, RGc], [Rcw[h]])
                            DVE(lambda e: e.tensor_tensor(out=sm[:, 8:16], in0=v8(Grow)[0:64, :, LAST], in1=gcol, op=ALU.subtract), [RGrow, RGc], [Rsm])
                            ACT(lambda e: e.activation(out=sm[:, 8:16], in_=sm[:, 8:16], func=AF.Exp), [Rsm], [Rsm])
                            pb, Rpb = npsb()
                            for n in range(8):
                                PE(lambda e, n=n, pb=pb, h=h: e.transpose(out=pb[0:64, n * 128:(n + 1) * 128], in_=Kt[h][:, n * 64:(n + 1) * 64], identity=identb[:]), [RKt[h], Ridb], [Rpb])
                            DVE(lambda e, pb=pb, h=h: e.tensor_tensor(out=Ktil[h][:], in0=pb[0:64, :].rearrange("p (n k) -> p n k", k=128), in1=bc_last(sm[:, 8:16], 128), op=ALU.mult), [Rpb, Rsm], [RKtil[h]])
                            pb, Rpb = npsb()
                            for n in range(8):
                                PE(lambda e, n=n, pb=pb: e.transpose(out=pb[0:64, n * 128:(n + 1) * 128], in_=vtb[:, n * 64:(n + 1) * 64], identity=identb[:]), [Rvtb, Ridb], [Rpb])
                            DVE(lambda e, pb=pb, h=h: e.tensor_tensor(out=Vb[h][:], in0=pb[0:64, :].rearrange("p (n k) -> p n k", k=128), in1=bc_last(bcol_, 128), op=ALU.mult), [Rpb, RGc], [RVb[h]])
                            pm1, Rpm1 = nps()
                            for n in range(8):
                                cs = slice(n * 64, (n + 1) * 64)
                                PE(lambda e, cs=cs, pm1=pm1, h=h: e.matmul(pm1[0:64, cs], Kt[h][:, cs], Kt[h][:, cs], start=True, stop=True), [RKt[h]], [Rpm1])
                            DVE(lambda e, pm1=pm1: e.tensor_tensor(out=X[0][:], in0=pm1[0:64, :], in1=DL[:], op=ALU.mult), [Rpm1, RDL], [RX[0]])
                            DVE(lambda e, pm1=pm1: e.tensor_tensor(out=Y[0][:], in0=pm1[0:64, :], in1=DLT[:], op=ALU.mult), [Rpm1, RDLT], [RY[0]])
                            pm2, Rpm2 = nps()
                            for n in range(8):
                                cs = slice(n * 64, (n + 1) * 64)
                                PE(lambda e, cs=cs, pm2=pm2, h=h: e.matmul(pm2[0:64, cs], Kt[h][:, cs], qtb[:, cs], start=True, stop=True), [RKt[h], Rqtb], [Rpm2])
                            DVE(lambda e, pm2=pm2, h=h: e.tensor_tensor(out=aT[h][:], in0=pm2[0:64, :], in1=DA[:], op=ALU.mult), [Rpm2, RDA], [RaT[h]])
                            DVE(lambda e: e.tensor_tensor(out=v8(Qc[0]), in0=v8(Y[0]), in1=bc_mid(I64, 8), op=ALU.add), [RY[0], Rcst], [RQc[0]])
                        cur = 0
                        for k in range(5):
                            nxt = 1 - cur
                            for h in range(4):
                                X, Y, Qc, RX, RY, RQc = XH[h], YH[h], QH[h], RXH[h], RYH[h], RQH[h]
                                pX, RpX = nps()
                                for n in range(8):
                                    cs = slice(n * 64, (n + 1) * 64)
                                    PE(lambda e, cs=cs, pX=pX, cur=cur, X=X, Y=Y: e.matmul(pX[0:64, cs], Y[cur][:, cs], X[cur][:, cs], start=True, stop=True), [RX[cur], RY[cur]], [RpX])
                                ACT(lambda e, pX=pX, nxt=nxt, X=X: e.activation(out=X[nxt][:], in_=pX[0:64, :], func=AF.Copy), [RpX], [RX[nxt]])
                                if k < 4:
                                    pY, RpY = nps()
                                    for n in range(8):
                                        cs = slice(n * 64, (n + 1) * 64)
                                        PE(lambda e, cs=cs, pY=pY, cur=cur, X=X, Y=Y: e.matmul(pY[0:64, cs], X[cur][:, cs], Y[cur][:, cs], start=True, stop=True), [RX[cur], RY[cur]], [RpY])
                                    ACT(lambda e, pY=pY, nxt=nxt, Y=Y: e.activation(out=Y[nxt][:], in_=pY[0:64, :], func=AF.Copy), [RpY], [RY[nxt]])
                            for h in range(4):
                                X, Y, Qc, RX, RY, RQc = XH[h], YH[h], QH[h], RXH[h], RYH[h], RQH[h]
                                pQ, RpQ = nps()
                                for n in range(8):
                                    cs = slice(n * 64, (n + 1) * 64)
                                    PE(lambda e, cs=cs, pQ=pQ, nxt=nxt, cur=cur, X=X, Qc=Qc: e.matmul(pQ[0:64, cs], X[nxt][:, cs], Qc[cur][:, cs], start=True, stop=True), [RX[nxt], RQc[cur]], [RpQ])
                                dstq = Qc[nxt] if k < 4 else Qf[h]
                                Rdq = RQc[nxt] if k < 4 else RQf[h]
                                DVE(lambda e, pQ=pQ, cur=cur, dstq=dstq, Qc=Qc: e.tensor_tensor(out=dstq[:], in0=pQ[0:64, :], in1=Qc[cur][:], op=ALU.add), [RpQ, RQc[cur]], [Rdq])
                            cur = nxt
                        chunks = range(8) if d == 0 else range(7, -1, -1)
                        for n in chunks:
                            cs = slice(n * 64, (n + 1) * 64)
                            tstart = t0 + n * 64
                            for h in range(4):
                                if (d == 0 and tstart == HALF) or (d == 1 and tstart + 64 == HALF):
                                    DVE(lambda e, h=h: e.tensor_scalar(out=S32[h][:], in0=S32[h][:], scalar1=cfg[:, 2:3], scalar2=None, op0=ALU.mult), [RS32[h], Rcfg], [RS32[h]])
                                    ACT(lambda e, h=h: e.activation(out=Sb[h][:], in_=S32[h][:], func=AF.Copy), [RS32[h]], [RSb[h]])
                                po, Rpo = psf[2 + h], Rpsf[2 + h]
                                p1, Rp1 = ring2()
                                PE(lambda e, p1=p1, h=h, cs=cs: e.matmul(p1[0:64, 0:128], Kt[h][:, cs], Sb[h][:], start=True, stop=True), [RKt[h], RSb[h]], [Rp1])
                                DVE(lambda e, p1=p1, h=h, n=n: e.scalar_tensor_tensor(out=Rm_[h][:], in0=p1[0:64, 0:128], scalar=cwc[h][:, n:n + 1], in1=Vb[h][:, n, :], op0=ALU.mult, op1=ALU.add), [Rp1, Rcw[h], RVb[h]], [RRm[h]])
                                p2, Rp2 = ring2()
                                PE(lambda e, p2=p2, h=h, cs=cs: e.matmul(p2[0:64, 0:128], Qf[h][:, cs], Rm_[h][:], start=True, stop=True), [RQf[h], RRm[h]], [Rp2])
                                ACT(lambda e, p2=p2, h=h: e.activation(out=vn[h][:], in_=p2[0:64, 0:128], func=AF.Copy), [Rp2], [Rvn[h]])
                                PE(lambda e, po=po, h=h, cs=cs: e.matmul(po[:, cs], Sb[h][:], Qg[h][:, cs], start=True, stop=False), [RSb[h], RQg[h]], [Rpo])
                                PE(lambda e, po=po, h=h, cs=cs: e.matmul(po[:, cs], vn[h][:], aT[h][:, cs], start=False, stop=True), [Rvn[h], RaT[h]], [Rpo])
                                p3, Rp3 = ring2()
                                PE(lambda e, p3=p3, h=h, n=n: e.matmul(p3[:, 0:128], Ktil[h][:, n, :], vn[h][:], start=True, stop=True), [RKtil[h], Rvn[h]], [Rp3])
                                DVE(lambda e, p3=p3, h=h, n=n: e.scalar_tensor_tensor(out=S32[h][:], in0=S32[h][:], scalar=El[h][:, n:n + 1], in1=p3[:, 0:128], op0=ALU.mult, op1=ALU.add), [Rp3, REl[h], RS32[h]], [RS32[h]])
                                ACT(lambda e, h=h: e.activation(out=Sb[h][:], in_=S32[h][:], func=AF.Copy), [RS32[h]], [RSb[h]])
                        for h in range(4):
                            po, Rpo = psf[2 + h], Rpsf[2 + h]
                            rows = slice(h * 128, (h + 1) * 128)
                            if d == 0:
                                DVE(lambda e, po=po: e.tensor_copy(out=osum[:], in_=po[:, :]), [Rpo], [Rosum])
                                store(ofT[rows, t0:t0 + 512], osum[:], Rosum, R["ofT"])
                            else:
                                load(ofs[:], ofT[rows, t0:t0 + 512], Rofs, src=[R["ofT"]])
                                load(zt[:], dqT[1536 + h * 128:1536 + (h + 1) * 128, t0:t0 + 512], Rzt, src=[R["dqT"]])
                                DVE(lambda e, po=po: e.tensor_tensor(out=osum[:], in0=po[:, :], in1=ofs[:], op=ALU.add), [Rpo, Rofs], [Rosum])
                                ACT(lambda e: e.activation(out=sqb[:, 0:512], in_=osum[:], func=AF.Square), [Rosum], [Rsqb])
                                pt, Rp = ring2()
                                PE(lambda e, pt=pt: e.matmul(pt[:, :], onesb[:], sqb[:, 0:512], start=True, stop=True), [Rones, Rsqb], [Rp])
                                ACT(lambda e, pt=pt: e.activation(out=rqk[:, 0:512], in_=pt[:, :], func=AF.Sqrt, scale=1.0 / 128, bias=EPSC[:, 0:1]), [Rp, REPS], [Rrqk])
                                DVE(lambda e: e.reciprocal(out=rqk[:, 0:512], in_=rqk[:, 0:512]), [Rrqk], [Rrqk])
                                DVE(lambda e: e.scalar_tensor_tensor(out=osum[:], in0=osum[:], scalar=ong[:, 0:1], in1=rqk[:, 0:512], op0=ALU.mult, op1=ALU.mult), [Rosum, Rrqk, Rdcol], [Rosum])
                                ACT(lambda e: e.activation(out=rqk[:, 512:1024], in_=zt[:], func=AF.Silu), [Rzt, Rrqk], [Rrqk])
                                DVE(lambda e: e.tensor_tensor(out=ostg[:], in0=osum[:], in1=rqk[:, 512:1024], op=ALU.mult), [Rosum, Rrqk], [Rostg])
                                store(oT[1536 + h * 128:1536 + (h + 1) * 128, t0:t0 + 512], ostg[:], Rostg, R["oT"])
            P.barrier()

        def phaseM(l, xsrc, xres, xdst, xdres):
            with ExitStack() as ps:
                T_ = lambda n, s, d=F32: ps.enter_context(sbt(n, list(s), d))
                TG = 256
                wg = T_("wg", [128, 8, 4096], BF16); RwA = P.dma_res("wM")
                for kc in range(8):
                    load(wg[:, kc, :], wb["w_in"][l, kc * 128:(kc + 1) * 128, 5040:9136], RwA, src=[R_wb])
                wbr = T_("wbr", [128, 16, 1024], BF16); wo = T_("wo", [128, 8, 1024], BF16)
                load(wbr[:], wb["w_branch"][l].rearrange("(k p) n -> p k n", p=128), RwA, src=[R_wb])
                load(wo[:], wb["w_out"][l].rearrange("(k p) n -> p k n", p=128), RwA, src=[R_wb])
                gt = T_("gt", [128, D]); Rg = P.dma_res("gtM")
                load(gt[:], W["ln1_g"][l:l + 1, :].broadcast_to([128, D]), Rg)
                xt = T_("xt", [128, 2, D]); Rxt = P.dma_res("xtM")
                xnb = T_("xnb", [128, 2, D], BF16); Rxnb = Res()
                xnT = T_("xnT", [128, 8, TG], BF16); RxnT = Res()
                junk = T_("junk", [128, D]); Rjunk = Res(); st = T_("st", [128, 24]); Rst = Res()
                oTt = T_("oTt", [128, 16, TG], BF16); RoTt = P.dma_res("oTt")
                sg = T_("sg", [128, TG]); Rsg = Res(); acc = T_("acc", [128, TG]); Racc = Res(); tmp = T_("tmpM", [128, TG]); Rtmp = Res()
                mT = T_("mT", [128, 8, TG], BF16); RmT = Res()
                xo = [T_("xo%d" % i, [128, 2, D]) for i in range(2)]; Rxo = [Res(), Res()]
                for g in range(T // TG):
                    t0 = g * TG; ob_ = g % 2
                    norm_T("M", xsrc, t0, TG, gt, Rg, xt, Rxt, xnb, Rxnb, xnT, RxnT, junk, Rjunk, st, Rst, [xres])
                    load(oTt[:], oT[:, t0:t0 + TG].rearrange("(k p) t -> p k t", p=128), RoTt, src=[R["oT"]])
                    for m in range(8):
                        for br in range(4):
                            pg, Rpg = nps()
                            for kc in range(8):
                                PE(lambda e, pg=pg, kc=kc, br=br, m=m: e.matmul(pg[:, 0:TG], wg[:, kc, br * 1024 + m * 128:br * 1024 + (m + 1) * 128], xnT[:, kc, :], start=(kc == 0), stop=(kc == 7)), [RwA, RxnT], [Rpg])
                            ACT(lambda e, pg=pg: e.activation(out=sg[:], in_=pg[:, 0:TG], func=AF.Sigmoid), [Rpg], [Rsg])
                            pb_, Rpb_ = nps()
                            for kc in range(4):
                                PE(lambda e, pb_=pb_, kc=kc, br=br, m=m: e.matmul(pb_[:, 0:TG], wbr[:, br * 4 + kc, m * 128:(m + 1) * 128], oTt[:, br * 4 + kc, :], start=(kc == 0), stop=(kc == 3)), [RwA, RoTt], [Rpb_])
                            if br == 0:
                                DVE(lambda e, pb_=pb_: e.tensor_tensor(out=acc[:], in0=pb_[:, 0:TG], in1=sg[:], op=ALU.mult), [Rpb_, Rsg], [Racc])
                            else:
                                DVE(lambda e, pb_=pb_: e.tensor_tensor(out=tmp[:], in0=pb_[:, 0:TG], in1=sg[:], op=ALU.mult), [Rpb_, Rsg], [Rtmp])
                                if br < 3:
                                    DVE(lambda e: e.tensor_tensor(out=acc[:], in0=acc[:], in1=tmp[:], op=ALU.add), [Racc, Rtmp], [Racc])
                                else:
                                    DVE(lambda e, m=m: e.tensor_tensor(out=mT[:, m, :], in0=acc[:], in1=tmp[:], op=ALU.add), [Racc, Rtmp], [RmT])
                    for j in range(TG // 128):
                        for nh in range(2):
                            pt, Rp = nps()
                            for kc in range(8):
                                PE(lambda e, pt=pt, kc=kc, j=j, nh=nh: e.matmul(pt[:, :], mT[:, kc, j * 128:(j + 1) * 128], wo[:, kc, nh * 512:(nh + 1) * 512], start=(kc == 0), stop=(kc == 7)), [RmT, RwA], [Rp])
                            DVE(lambda e, pt=pt, j=j, nh=nh, ob_=ob_: e.tensor_tensor(out=xo[ob_][:, j, nh * 512:(nh + 1) * 512], in0=pt[:, :], in1=xt[:, j, nh * 512:(nh + 1) * 512], op=ALU.add), [Rp, Rxt], [Rxo[ob_]])
                    store(xdst[t0:t0 + TG, :].rearrange("(j p) d -> p j d", p=128), xo[ob_][:], Rxo[ob_], xdres)
            P.barrier()

        def phaseF(l, xsrc, xres, xdst, xdres):
            with ExitStack() as ps:
                T_ = lambda n, s, d=F32: ps.enter_context(sbt(n, list(s), d))
                TG = 256
                wu = T_("wu", [128, 8, 2 * FF], BF16); RwA = P.dma_res("wF")
                for kc in range(8):
                    load(wu[:, kc, :], wb["w_up"][l, kc * 128:(kc + 1) * 128, :], RwA, src=[R_wb])
                wd = T_("wd", [128, 22, 1024], BF16)
                load(wd[:], wb["w_down"][l].rearrange("(k p) n -> p k n", p=128), RwA, src=[R_wb])
                gt = T_("gt", [128, D]); Rg = P.dma_res("gtF")
                load(gt[:], W["ln2_g"][l:l + 1, :].broadcast_to([128, D]), Rg)
                xt = T_("xt", [128, 2, D]); Rxt = P.dma_res("xtF")
                xnb = T_("xnb", [128, 2, D], BF16); Rxnb = Res()
                xnT = T_("xnT", [128, 8, TG], BF16); RxnT = Res()
                junk = T_("junk", [128, D]); Rjunk = Res(); st = T_("st", [128, 24]); Rst = Res()
                sg = [T_("sgF%d" % i, [128, TG]) for i in range(2)]; Rsg = [Res(), Res()]
                hT = T_("hT", [128, 22, TG], BF16); RhT = Res()
                xo = [T_("xo%d" % i, [128, 2, D]) for i in range(2)]; Rxo = [Res(), Res()]
                for g in range(T // TG):
                    t0 = g * TG; ob_ = g % 2
                    norm_T("F", xsrc, t0, TG, gt, Rg, xt, Rxt, xnb, Rxnb, xnT, RxnT, junk, Rjunk, st, Rst, [xres])
                    for c in range(22):
                        pg, Rpg = nps()
                        for kc in range(8):
                            PE(lambda e, pg=pg, kc=kc, c=c: e.matmul(pg[:, 0:TG], wu[:, kc, c * 128:(c + 1) * 128], xnT[:, kc, :], start=(kc == 0), stop=(kc == 7)), [RwA, RxnT], [Rpg])
                        pu, Rpu = nps()
                        for kc in range(8):
                            PE(lambda e, pu=pu, kc=kc, c=c: e.matmul(pu[:, 0:TG], wu[:, kc, FF + c * 128:FF + (c + 1) * 128], xnT[:, kc, :], start=(kc == 0), stop=(kc == 7)), [RwA, RxnT], [Rpu])
                        sb_ = c % 2
                        ACT(lambda e, pg=pg, sb_=sb_: e.activation(out=sg[sb_][:], in_=pg[:, 0:TG], func=AF.Silu), [Rpg], [Rsg[sb_]])
                        DVE(lambda e, pu=pu, c=c, sb_=sb_: e.tensor_tensor(out=hT[:, c, :], in0=pu[:, 0:TG], in1=sg[sb_][:], op=ALU.mult), [Rpu, Rsg[sb_]], [RhT])
                    for j in range(TG // 128):
                        for nh in range(2):
                            pt, Rp = nps()
                            for kc in range(22):
                                PE(lambda e, pt=pt, kc=kc, j=j, nh=nh: e.matmul(pt[:, :], hT[:, kc, j * 128:(j + 1) * 128], wd[:, kc, nh * 512:(nh + 1) * 512], start=(kc == 0), stop=(kc == 21)), [RhT, RwA], [Rp])
                            DVE(lambda e, pt=pt, j=j, nh=nh, ob_=ob_: e.tensor_tensor(out=xo[ob_][:, j, nh * 512:(nh + 1) * 512], in0=pt[:, :], in1=xt[:, j, nh * 512:(nh + 1) * 512], op=ALU.add), [Rp, Rxt], [Rxo[ob_]])
                    store(xdst[t0:t0 + TG, :].rearrange("(j p) d -> p j d", p=128), xo[ob_][:], Rxo[ob_], xdres)
            P.barrier()

        Rxin = P.dma_res("x_in_dummy")
        phase0()
        cur, curR = x_in, Rxin
        for l in range(NL):
            run = (lambda nm: phases is None or nm in phases)
            if run("A1"): phaseA1(l, cur, curR)
            if run("A2"): phaseA2(l, cur, curR)
            if run("mla"):
                attention("b", lambda h: bqT[h], lambda h: bkT[h], lambda h: bv[:, h * 64:(h + 1) * 64], R["bqT"], R["bkT"], R["bv"], 96, 96.0 ** -0.5, 512, False)
            if run("dil"):
                attention("a", lambda h: aqT[h // 2, (h % 2) * 64:(h % 2) * 64 + 64, :], lambda h: akT[h // 2, (h % 2) * 64:(h % 2) * 64 + 64, :],
                          lambda h: av[:, h * 64:(h + 1) * 64], R["aqT"], R["akT"], R["av"], 64, 0.125, 0, True)
            if run("lru"): phaseC_rglru(l)
            if run("gdn"): phaseD_gdn(l)
            if run("M"): phaseM(l, cur, curR, x1, R["x1"])
            last = (l == NL - 1)
            dst, dR = (y_out, R["y"]) if last else (x2, R["x2"])
            if run("F"): phaseF(l, x1, R["x1"], dst, dR)
            cur, curR = x2, R["x2"]
        P.barrier()
    return nc, P


def _consts(T):
    HALF = T // 2
    cst = np.zeros((128, 512), np.float32)
    cst[:, 0:128] = np.eye(128, dtype=np.float32)
    p = np.arange(64)[:, None]; f = np.arange(64)[None, :]
    cst[0:64, 128:192] = (p > f); cst[0:64, 192:256] = (p >= f)
    cst[0:64, 256:320] = (p < f); cst[0:64, 320:384] = (p <= f)
    cst[0:64, 384:448] = np.eye(64, dtype=np.float32)
    cst[64:128, 448:512] = np.eye(64, dtype=np.float32)
    pp = np.arange(128)[:, None]; ff = np.arange(512)[None, :]
    am = np.zeros((20, 128, 512), np.float32)
    for m in range(20):
        rel = 128 * m - 1024 + pp - ff
        a = np.abs(rel)
        am[m] = (a <= 64).astype(np.float32) + ((rel % 4 == 0) & (a <= 256)) + ((rel % 16 == 0) & (a <= 1024))
    return cst, am


def _rope(pos, dim):
    inv = (1.0 / (np.float32(500000.0) ** (np.arange(0, dim, 2, dtype=np.float32) / np.float32(dim)))).astype(np.float32)
    ang = pos.astype(np.float32)[:, None] * inv[None, :]
    return np.concatenate([np.cos(ang), np.sin(ang)], axis=1).astype(np.float32)


_CACHE = {}


def run_units(units, links, T, weights, debug=False, phases=None, NL=2, ncores=8):
    key = (T, tuple(debug) if debug else None, tuple(phases) if phases else None, NL)
    if key not in _CACHE:
        _CACHE[key] = build(T, NL=NL, debug=debug, phases=phases)
    nc, P = _CACHE[key]
    cst, am = _consts(T)
    HALF = T // 2
    in_maps = []
    for c in range(ncores):
        u = c if c < len(units) else 0
        link = float(links[u])
        cfg = np.zeros((128, 4), np.float32)
        cfg[:, 1] = 0.0 if link else NEG
        cfg[:, 2] = link
        pos = np.arange(T) if link else np.concatenate([np.arange(HALF), np.arange(HALF)])
        m = {"x": np.ascontiguousarray(units[u], dtype=np.float32), "cfg": cfg, "ropeA": _rope(pos, 16), "ropeB": _rope(pos, 32),
             "amask": am, "cst": cst}
        for n, s in WSPECS:
            m[n] = np.ascontiguousarray(weights[n], dtype=np.float32).reshape(s)
        in_maps.append(m)
    res = run_bass_kernel_spmd(nc, in_maps, core_ids=list(range(ncores)))
    return res.results


def kernel(**inputs):
    xp = np.asarray(inputs["x_prompt"], dtype=np.float32)
    xs = np.asarray(inputs["x_sample"], dtype=np.float32)
    T = xs.shape[1]
    units = [xp[2 * i:2 * i + 2].reshape(T, D) for i in range(xp.shape[0] // 2)] + [xs[i] for i in range(xs.shape[0])]
    links = [0.0] * (xp.shape[0] // 2) + [1.0] * xs.shape[0]
    weights = {n: inputs[n] for n, _ in WSPECS}
    r = run_units(units, links, T, weights)
    npair = xp.shape[0] // 2
    yp = np.stack([r[i]["y"] for i in range(npair)], 0).reshape(xp.shape)
    ys = np.stack([r[npair + i]["y"] for i in range(xs.shape[0])], 0)
    return (yp.astype(np.float32), ys.astype(np.float32))
```
